# Optimizing a Trainium2 kernel written in Bass

```python
import jax, jax.numpy as jnp
from jax import lax
import numpy as np

D_MODEL = 1024
BATCH = 8
SEQ = 4096
DEPTH = 1

CHUNK = 64
ATT_HEADS = 8
ATT_HEAD_DIM = 64
ATT_WIDTH = ATT_HEADS * ATT_HEAD_DIM
N_PREV_CHUNKS = 8
BAND_CHUNKS = N_PREV_CHUNKS + 1
REL_CLIP = 256
SSD_HEAD_DIM = 64
SSD_WIDTH = D_MODEL
SSD_HEADS = SSD_WIDTH // SSD_HEAD_DIM
SSD_GROUPS = 2
SSD_STATE = 128
SSD_CONV = 4
SSD_CHUNK = CHUNK
CONV_DIM = SSD_WIDTH + 2 * SSD_GROUPS * SSD_STATE
MIX_WIDTH = ATT_WIDTH + SSD_WIDTH
IN_PROJ = 3 * ATT_WIDTH + SSD_WIDTH + CONV_DIM + SSD_HEADS
SPLITS = [ATT_WIDTH, 2 * ATT_WIDTH, 3 * ATT_WIDTH,
          3 * ATT_WIDTH + SSD_WIDTH,
          3 * ATT_WIDTH + SSD_WIDTH + CONV_DIM]
D_FF = 4 * D_MODEL
EPS = 1e-5

kernel_name = "hymba_chunkattn_ssd_sqrelu_block"


def rmsnorm(x, w):
    xf = x.astype(jnp.float32)
    y = xf * lax.rsqrt(jnp.mean(xf * xf, axis=-1, keepdims=True) + EPS)
    return (y * w.astype(jnp.float32)).astype(x.dtype)


def chunk_band_attention(q, k, v, rel_bias):
    bsz, seq = q.shape[0], q.shape[1]
    nc = seq // CHUNK
    shp = (bsz, nc, CHUNK, ATT_HEADS, ATT_HEAD_DIM)
    q = q.reshape(shp)
    k = k.reshape(shp)
    v = v.reshape(shp)
    pad = ((0, 0), (N_PREV_CHUNKS, 0), (0, 0), (0, 0), (0, 0))
    kp = jnp.pad(k, pad)
    vp = jnp.pad(v, pad)
    kb = jnp.concatenate([kp[:, j:j + nc] for j in range(BAND_CHUNKS)], axis=2)
    vb = jnp.concatenate([vp[:, j:j + nc] for j in range(BAND_CHUNKS)], axis=2)
    scale = ATT_HEAD_DIM ** -0.5
    scores = jnp.einsum('bcqhd,bckhd->bhcqk', q, kb).astype(jnp.float32) * scale
    qi = jnp.arange(CHUNK)[:, None]
    kj = jnp.arange(BAND_CHUNKS * CHUNK)[None, :]
    rel = N_PREV_CHUNKS * CHUNK + qi - kj
    idx = jnp.clip(rel, -REL_CLIP, REL_CLIP) + REL_CLIP
    bias = jnp.transpose(rel_bias.astype(jnp.float32)[idx], (2, 0, 1))
    key_chunk = jnp.arange(nc)[:, None] - N_PREV_CHUNKS + jnp.arange(BAND_CHUNKS)[None, :]
    valid = jnp.repeat(key_chunk >= 0, CHUNK, axis=1)
    scores = scores + bias[None, :, None]
    scores = jnp.where(valid[None, None, :, None, :], scores, -1e30)
    probs = jax.nn.softmax(scores, axis=-1).astype(v.dtype)
    out = jnp.einsum('bhcqk,bckhd->bcqhd', probs, vb)
    return out.reshape(bsz, seq, ATT_WIDTH)


def causal_depthwise_conv(u, w, b):
    out = lax.conv_general_dilated(
        u, w[:, None, :].astype(u.dtype), window_strides=(1,),
        padding=[(SSD_CONV - 1, 0)], dimension_numbers=('NWC', 'WIO', 'NWC'),
        feature_group_count=u.shape[-1])
    return out + b.astype(u.dtype)


def ssd_scan(x, dt, A, Bm, Cm):
    bsz, seq = x.shape[0], x.shape[1]
    nc = seq // SSD_CHUNK
    L = SSD_CHUNK
    r = SSD_HEADS // SSD_GROUPS
    X = (x * dt[..., None]).reshape(bsz, nc, L, SSD_GROUPS, r, SSD_HEAD_DIM)
    a = jnp.moveaxis((dt * A).reshape(bsz, nc, L, SSD_GROUPS, r), 2, -1)
    a_cum = jnp.cumsum(a, axis=-1)
    Bc = Bm.reshape(bsz, nc, L, SSD_GROUPS, SSD_STATE)
    Cc = Cm.reshape(bsz, nc, L, SSD_GROUPS, SSD_STATE)
    tril = jnp.tril(jnp.ones((L, L), dtype=bool))
    seg = a_cum[..., :, None] - a_cum[..., None, :]
    decay_in = jnp.exp(jnp.where(tril, seg, -jnp.inf))
    cb = jnp.einsum('bclgn,bcsgn->bcgls', Cc, Bc)
    y_diag = jnp.einsum('bcgls,bcgrls,bcsgrp->bclgrp', cb, decay_in, X)
    decay_out = jnp.exp(a_cum[..., -1:] - a_cum)
    states = jnp.einsum('bcsgn,bcgrs,bcsgrp->bcgrpn', Bc, decay_out, X)
    chunk_decay = jnp.exp(a_cum[..., -1])

    def step(carry, inp):
        st, dec = inp
        return carry * dec[..., None, None] + st, carry

    init = jnp.zeros_like(states[:, 0])
    _, prev = lax.scan(step, init, (jnp.moveaxis(states, 1, 0), jnp.moveaxis(chunk_decay, 1, 0)))
    prev = jnp.moveaxis(prev, 0, 1)
    y_off = jnp.einsum('bclgn,bcgrpn,bcgrl->bclgrp', Cc, prev, jnp.exp(a_cum))
    return (y_diag + y_off).reshape(bsz, seq, SSD_HEADS, SSD_HEAD_DIM)


def ssd_mixer(z, xbc, dt_raw, conv_w, conv_b, dt_bias, a_log, d_skip, norm_w):
    bsz, seq = z.shape[0], z.shape[1]
    xbc = jax.nn.silu(causal_depthwise_conv(xbc, conv_w, conv_b)).astype(jnp.float32)
    xs = xbc[..., :SSD_WIDTH]
    Bm = xbc[..., SSD_WIDTH:SSD_WIDTH + SSD_GROUPS * SSD_STATE].reshape(bsz, seq, SSD_GROUPS, SSD_STATE)
    Cm = xbc[..., SSD_WIDTH + SSD_GROUPS * SSD_STATE:].reshape(bsz, seq, SSD_GROUPS, SSD_STATE)
    dt = jax.nn.softplus(dt_raw.astype(jnp.float32) + dt_bias.astype(jnp.float32))
    A = -jnp.exp(a_log.astype(jnp.float32))
    xh = xs.reshape(bsz, seq, SSD_HEADS, SSD_HEAD_DIM)
    y = ssd_scan(xh, dt, A, Bm, Cm) + d_skip.astype(jnp.float32)[:, None] * xh
    y = y.reshape(bsz, seq, SSD_WIDTH) * jax.nn.silu(z.astype(jnp.float32))
    yg = y.reshape(bsz, seq, SSD_GROUPS, SSD_WIDTH // SSD_GROUPS)
    yg = yg * lax.rsqrt(jnp.mean(yg * yg, axis=-1, keepdims=True) + EPS)
    y = yg.reshape(bsz, seq, SSD_WIDTH) * norm_w.astype(jnp.float32)
    return y.astype(z.dtype)


def setup_inputs(seed: int = 0) -> dict:
    key = jax.random.key(seed)
    ks = jax.random.split(key, 16)
    f32 = jnp.float32
    x = jax.random.normal(ks[0], (BATCH, SEQ, D_MODEL), f32)
    norm_mix_w = 1.0 + 0.05 * jax.random.normal(ks[1], (DEPTH, D_MODEL), f32)
    w_in = jax.random.normal(ks[2], (DEPTH, D_MODEL, IN_PROJ), f32) * D_MODEL ** -0.5
    rel_bias = 0.1 * jax.random.normal(ks[3], (DEPTH, 2 * REL_CLIP + 1, ATT_HEADS), f32)
    conv_w = jax.random.normal(ks[4], (DEPTH, SSD_CONV, CONV_DIM), f32) * SSD_CONV ** -0.5
    conv_b = 0.01 * jax.random.normal(ks[5], (DEPTH, CONV_DIM), f32)
    dt0 = jnp.exp(jax.random.uniform(ks[6], (DEPTH, SSD_HEADS), f32,
                                     minval=np.log(1e-3), maxval=np.log(1e-1)))
    dt_bias = dt0 + jnp.log(-jnp.expm1(-dt0))
    a_log = jnp.log(jax.random.uniform(ks[7], (DEPTH, SSD_HEADS), f32, minval=1.0, maxval=16.0))
    d_skip = 1.0 + 0.1 * jax.random.normal(ks[8], (DEPTH, SSD_HEADS), f32)
    ssd_norm_w = 1.0 + 0.05 * jax.random.normal(ks[9], (DEPTH, SSD_WIDTH), f32)
    w_out = jax.random.normal(ks[10], (DEPTH, MIX_WIDTH, D_MODEL), f32) * MIX_WIDTH ** -0.5
    norm_mlp_w = 1.0 + 0.05 * jax.random.normal(ks[11], (DEPTH, D_MODEL), f32)
    w_ff1 = jax.random.normal(ks[12], (DEPTH, D_MODEL, D_FF), f32) * D_MODEL ** -0.5
    w_ff2 = jax.random.normal(ks[13], (DEPTH, D_FF, D_MODEL), f32) * D_FF ** -0.5
    norm_final_w = 1.0 + 0.05 * jax.random.normal(ks[14], (D_MODEL,), f32)
    return {"x": x, "norm_mix_w": norm_mix_w, "w_in": w_in, "rel_bias": rel_bias,
            "conv_w": conv_w, "conv_b": conv_b, "dt_bias": dt_bias, "a_log": a_log,
            "d_skip": d_skip, "ssd_norm_w": ssd_norm_w, "w_out": w_out,
            "norm_mlp_w": norm_mlp_w, "w_ff1": w_ff1, "w_ff2": w_ff2,
            "norm_final_w": norm_final_w}


def reference(x, norm_mix_w, w_in, rel_bias, conv_w, conv_b, dt_bias, a_log, d_skip,
              ssd_norm_w, w_out, norm_mlp_w, w_ff1, w_ff2, norm_final_w):
    h = x
    for i in range(DEPTH):
        hn = rmsnorm(h, norm_mix_w[i])
        proj = hn @ w_in[i]
        q, k, v, z, xbc, dt_raw = jnp.split(proj, SPLITS, axis=-1)
        att = chunk_band_attention(q, k, v, rel_bias[i])
        ssd = ssd_mixer(z, xbc, dt_raw, conv_w[i], conv_b[i], dt_bias[i],
                        a_log[i], d_skip[i], ssd_norm_w[i])
        h = h + jnp.concatenate([att, ssd], axis=-1) @ w_out[i]
        hn = rmsnorm(h, norm_mlp_w[i])
        h = h + jnp.square(jax.nn.relu(hn @ w_ff1[i])) @ w_ff2[i]
    return rmsnorm(h, norm_final_w)
```

```python
import numpy as np
from contextlib import ExitStack
import concourse.bass as bass
import concourse.mybir as mybir
from concourse.bass_utils import run_bass_kernel_spmd

F32 = mybir.dt.float32
BF16 = mybir.dt.bfloat16
AF = mybir.ActivationFunctionType
ALU = mybir.AluOpType

D = 1024
INP = 4112
EPS = 1e-5
NSLOT = 3
import os
FFN2_VIA_POOL = int(os.environ.get('FFN2_VIA_POOL', '0'))
LEAD = float(os.environ.get('LEAD', '1.15'))
NOTILE = int(os.environ.get('NOTILE', '0'))
SEQ_MODE = int(os.environ.get('SEQ_MODE', '0'))
ENGS = ("pe", "act", "dve", "pool", "sp")


class Op:
    __slots__ = ("eng", "emit", "deps", "chan", "val", "milestone", "ms", "final")

    def __init__(self, eng, emit, chan):
        self.eng, self.emit, self.chan = eng, emit, chan
        self.deps = set()
        self.val = 0
        self.milestone = False
        self.ms = 0
        self.final = False


class Rec:
    def __init__(self):
        self.ops = {e: [] for e in ENGS}
        self.lastw = {}
        self.readers = {}
        self.chan = {}
        self.final_chans = set()

    def op(self, eng, emit, reads=(), writes=(), chan=None, final=False):
        o = Op(eng, emit, chan)
        deps = set()
        for r in reads:
            w = self.lastw.get(r)
            if w is not None:
                deps.add(w)
        for w_ in writes:
            w = self.lastw.get(w_)
            if w is not None:
                deps.add(w)
            for rd in self.readers.get(w_, {}).values():
                deps.add(rd)
        if eng == "pe":
            deps = {d for d in deps if not (d.eng == "pe" and d.chan is None)}
        o.deps = deps
        key = eng if chan is None else ("dma", chan)
        for r in reads:
            self.readers.setdefault(r, {})[key] = o
        for w_ in writes:
            self.lastw[w_] = o
            self.readers[w_] = {}
        if chan is not None:
            n = self.chan.get(chan, 0) + 1
            self.chan[chan] = n
            o.val = 16 * n
            o.final = final
            if final:
                self.final_chans.add(chan)
        self.ops[eng].append(o)
        return o

    def emit_all(self, nc, es):
        for e in ENGS:
            for o in self.ops[e]:
                for d in o.deps:
                    d.milestone = True
        engsem = {e: es.enter_context(nc.semaphore("sem_" + e)) for e in ENGS}
        chansem = {c: es.enter_context(nc.semaphore("ch_" + str(c))) for c in self.chan}
        for e in ENGS:
            k = 0
            for o in self.ops[e]:
                if o.chan is None and o.milestone:
                    k += 1
                    o.ms = k
        block = es.enter_context(nc.Block())
        rec = self

        def run(e, eng):
            waited = {}
            for o in rec.ops[e]:
                for d in o.deps:
                    if d.chan is not None:
                        sem = chansem[d.chan]
                        val = 16 * rec.chan[d.chan] if d.final else d.val
                    else:
                        sem = engsem[d.eng]
                        val = d.ms
                    kk = sem.num
                    if waited.get(kk, 0) < val:
                        eng.wait_ge(sem, val)
                        waited[kk] = val
                ins = o.emit(eng)
                if o.chan is not None:
                    ins.then_inc(chansem[o.chan], 16)
                elif o.milestone:
                    ins.then_inc(engsem[e], 1)

        @block.sync
        def _(eng):
            run("sp", eng)

        @block.tensor
        def _(eng):
            run("pe", eng)

        @block.scalar
        def _(eng):
            run("act", eng)

        @block.vector
        def _(eng):
            run("dve", eng)

        @block.gpsimd
        def _(eng):
            run("pool", eng)


def build_program(NT=8, debug=False):
    _, _, plan = _build(NT, debug, None)
    nc, dbg, _ = _build(NT, debug, plan)
    return nc, dbg


def _build(NT, debug, plan):
    S = NT * 512
    nc = bass.Bass("TRN2", target_bir_lowering=False)
    R = Rec()
    plan_out = {"seq": [], "counts": {}}

    def din(name, shape, dt=F32):
        return nc.dram_tensor(name, list(shape), dt, kind="ExternalInput").ap()

    x = din("x", [S, D])
    w_in = din("w_in", [D, INP])
    w_out = din("w_out", [1536, D])
    w_ff1 = din("w_ff1", [D, 4096])
    w_ff2 = din("w_ff2", [4096, D])
    vecs = din("vecs", [128, 84])
    rows = din("rows", [1, 2608])
    btab = din("btab", [128, 8 * 640])
    consts = din("consts", [128, 384])
    y = nc.dram_tensor("y", [S, D], F32, kind="ExternalOutput").ap()
    wb_in = nc.dram_tensor("wb_in", [D, 4096], BF16, kind="Internal").ap()
    wb_out = nc.dram_tensor("wb_out", [1536, D], BF16, kind="Internal").ap()
    wb_ff1 = nc.dram_tensor("wb_ff1", [D, 4096], BF16, kind="Internal").ap()
    wb_ff2 = nc.dram_tensor("wb_ff2", [4096, D], BF16, kind="Internal").ap()
    dbg = {}

    def dout(name, shape):
        a = nc.dram_tensor(name, list(shape), F32, kind="ExternalOutput").ap()
        dbg[name] = a
        return a

    es = ExitStack()
    with es:
        def sb(name, shape, dt):
            return es.enter_context(nc.sbuf_tensor(name, list(shape), dt))

        ring = sb("ring", [128, NSLOT, 8, 512], BF16)
        h = sb("h", [128, 4, 1024], F32)
        hnT = sb("hnT", [128, 8, 512], BF16)
        qT2 = sb("qT2", [128, 2, 4, 512], BF16)
        kT = sb("kT", [128, 4, 1024], BF16)
        Vaug = sb("Vaug", [128, 8, 8, 65], BF16)
        zs = sb("zs", [128, 4, 1024], BF16)
        ovl = sb("ovl", [128, 11312], BF16)
        uT = ovl[:, 0:6192].rearrange("p (c n) -> p c n", c=12)
        xs_tm = ovl[:, 6192:11312].rearrange("p (b n) -> p b n", b=4)
        hidT = sb("hidT", [128, 8, 512], BF16)
        xst = hidT[:].rearrange("p a b -> p (a b)").bitcast(F32).rearrange("p (b n) -> p b n", b=2)
        BCT = sb("BCT", [128, 4, 512], BF16)
        AU = sb("AU", [128, 2, 8, 128], BF16)
        DT = sb("DT", [128, 16, 128], BF16)
        MT = sb("MT", [128, 16, 128], BF16)
        CBm = sb("CBm", [128, 2, 128], BF16)
        Xdt = sb("Xdt", [128, 1024], BF16)
        Xd = sb("Xd", [128, 1024], BF16)
        Sst = sb("Sst", [128, 1024], F32)
        Sbf = sb("Sbf", [128, 1024], BF16)
        ysb = sb("ysb", [128, 1024], F32)
        yn = sb("yn", [128, 1024], BF16)
        Xsk = sb("Xsk", [128, 1024], BF16)
        E = sb("E", [128, 2, 5, 128], BF16)
        att_tm = sb("att_tm", [128, 2, 512], BF16)
        mixT = sb("mixT", [128, 12, 512], BF16)
        relu = sb("relu", [128, 2, 512], F32)
        expBT = sb("expBT", [128, 8, 5, 128], BF16)
        diagW = sb("diagW", [128, 12, 4, 128], BF16)
        cst = sb("cst", [128, 384], F32)
        vec = sb("vec", [128, 84], F32)
        rowb = sb("rowb", [128, 1072], F32)
        convb_bf = sb("convb_bf", [1, 1536], BF16)
        ones_bf = sb("ones_bf", [1, 128], BF16)
        ones_f = sb("ones_f", [128, 128], F32)
        ident_bf = sb("ident_bf", [128, 128], BF16)
        Tm_bf = sb("Tm_bf", [128, 128], BF16)
        Um_bf = sb("Um_bf", [128, 128], BF16)
        wdt = sb("wdt", [128, 8, 16], BF16)
        junk = sb("junk", [128, 1024], BF16)
        xsbf = sb("xsbf", [128, 2, 1024], BF16)
        small = sb("small", [128, 256], F32)
        dtt = sb("dtt", [128, 4, 16], F32)
        aa = sb("aa", [128, 4, 16], F32)
        ahl = sb("ahl", [128, 2, 16], BF16)
        uhist = sb("uhist", [128, 12, 3], BF16)
        eacd4 = sb("eacd4", [128, 4, 32], F32)
        w24 = sb("w24", [128, 4, 16], F32)
        raw4 = sb("raw4", [128, 4, 32], F32)
        ahl4 = sb("ahl4", [128, 2, 4, 16], BF16)
        PS = es.enter_context(nc.psum_tensor("PS", [128, 4096], F32))

        ident_f = cst[:, 0:128]
        Tm_f = cst[:, 128:256]
        Um_f = cst[:, 256:384]
        nfw_bc = rowb[:, 0:1024]
        dtb_bc = rowb[:, 1024:1040]
        alog_bc = rowb[:, 1040:1056]
        dsk_bc = rowb[:, 1056:1072]
        ssq = small[:, 0:4]
        rstd = small[:, 4:8]
        gss = small[:, 8:10]
        grstd = small[:, 10:12]
        epsc = small[:, 12:13]
        onec = small[:, 13:14]
        A_bc = small[:, 16:32]
        eacd = small[:, 32:64]
        w2 = small[:, 64:80]
        rc = small[:, 80:82]
        adiff = small[:, 160:176]
        eacd2 = small[:, 176:240].rearrange("p (b n) -> p b n", b=2)
        w22 = small[:, 32:64].rearrange("p (b n) -> p b n", b=2)
        dtr = small[:, 96:160].rearrange("p (s n) -> p s n", s=4)

        def bank(b):
            return PS[:, b * 512:(b + 1) * 512]

        def pres(b):
            return ["ps%d" % b]

        def bank_bf(b):
            return PS[:, b * 512:(b + 1) * 512].bitcast(BF16)

        def mm_group(items, reads, writes):
            def emit(e):
                ins = None
                for (o, l, r, st, sp) in items:
                    ins = e.matmul(o, lhsT=l, rhs=r, start=st, stop=sp, skip_group_check=True)
                return ins
            return R.op("pe", emit, reads, writes)

        def tr_group(items, reads, writes):
            def emit(e):
                ins = None
                for (o, i) in items:
                    ins = e.transpose(out=o, in_=i, identity=ident_bf[:])
                return ins
            return R.op("pe", emit, reads, writes)

        def act(out, in_, func, reads, writes, **kw):
            return R.op("act", lambda e: e.activation(out=out, in_=in_, func=func, **kw), reads, writes)

        def tt(eng, out, in0, in1, op, reads, writes):
            return R.op(eng, lambda e: e.tensor_tensor(out=out, in0=in0, in1=in1, op=op), reads, writes)

        def tsc(eng, out, in0, s1, s2, op0, op1, reads, writes):
            if s2 is None:
                return R.op(eng, lambda e: e.tensor_scalar(out=out, in0=in0, scalar1=s1, scalar2=None, op0=op0), reads, writes)
            return R.op(eng, lambda e: e.tensor_scalar(out=out, in0=in0, scalar1=s1, scalar2=s2, op0=op0, op1=op1), reads, writes)

        def cp(eng, out, in_, reads, writes):
            if eng == "act":
                return R.op(eng, lambda e: e.activation(out=out, in_=in_, func=AF.Copy), reads, writes)
            return R.op(eng, lambda e: e.tensor_copy(out=out, in_=in_), reads, writes)

        def mset(eng, ap, val, writes):
            return R.op(eng, lambda e: e.memset(ap, val), (), writes)

        def dma(eng, out, in_, reads, writes, chan, final=False):
            return R.op(eng, lambda e: e.dma_start(out=out, in_=in_), reads, writes, chan=chan, final=final)

        def run(gen):
            for _ in gen:
                pass

        WAIT = "WAIT"

        def interleave(key, streams):
            names = list(streams)
            cnt = {n: 0 for n in names}
            alive = dict(streams)
            tot = plan["counts"].get(key) if plan else None
            while alive:
                if tot:
                    order = sorted(alive, key=lambda k: (cnt[k] + 0.5) / max(tot[k] * (1.0 if k == "ffn" else LEAD), 1))
                else:
                    order = sorted(alive, key=lambda k: cnt[k])
                progressed = False
                for n in order:
                    try:
                        r = next(alive[n])
                    except StopIteration:
                        del alive[n]
                        progressed = True
                        break
                    if r is WAIT:
                        continue
                    cnt[n] += 1
                    progressed = True
                    break
                assert progressed, ("interleave deadlock", key, list(alive))
            plan_out["counts"][key] = cnt

        def wsrc(key):
            kind = key[0]
            if kind == "in":
                j = key[1]
                return wb_in[:, j * 512:(j + 1) * 512].rearrange("(c p) n -> p c n", p=128), 8, f"wbin{j}"
            if kind == "out":
                half, part = key[1], key[2]
                if part == 0:
                    return wb_out[0:1024, half * 512:(half + 1) * 512].rearrange("(c p) n -> p c n", p=128), 8, f"wbout{half}a"
                return wb_out[1024:1536, half * 512:(half + 1) * 512].rearrange("(c p) n -> p c n", p=128), 4, f"wbout{half}b"
            if kind == "ff1":
                fs = key[1]
                return wb_ff1[:, fs * 512:(fs + 1) * 512].rearrange("(c p) n -> p c n", p=128), 8, f"wbff1{fs}"
            fg, half = key[1], key[2]
            return wb_ff2[fg * 1024:(fg + 1) * 1024, half * 512:(half + 1) * 512].rearrange("(c p) n -> p c n", p=128), 8, f"wbff2{fg}{half}"

        wst = {"next_load": 0, "next_use": 0, "rest_ready": False}

        def issue_load(li, key):
            src, nk, res = wsrc(key)
            sl = li % NSLOT
            dma("sp", ring[:, sl, 0:nk, :], src, [res], [f"ring{sl}"], chan=f"ring{sl}")

        def use_slot(key):
            idx = wst["next_use"]
            wst["next_use"] += 1
            plan_out["seq"].append(key)
            if plan:
                seq = plan["seq"]
                assert seq[idx] == key, (idx, key, seq[idx])
                while wst["next_load"] <= idx + NSLOT - 1 and wst["next_load"] < len(seq):
                    if seq[wst["next_load"]][0] != "in" and not wst["rest_ready"]:
                        break
                    issue_load(wst["next_load"], seq[wst["next_load"]])
                    wst["next_load"] += 1
            else:
                issue_load(idx, key)
            sl = idx % NSLOT
            return ring[:, sl], f"ring{sl}"

        def conv_weights(which, gate):
            if which == "in":
                for j in range(8):
                    dma("pool", wb_in[:, j * 512:(j + 1) * 512], w_in[:, j * 512:(j + 1) * 512], gate, [f"wbin{j}"], chan=f"cv_in{j}")
                dma("pool", wdt[:], w_in[:, 4096:4112].rearrange("(c p) n -> p c n", p=128), [], ["wdt"], chan="cv_dt")
                return
            for half in range(2):
                dma("pool", wb_out[0:1024, half * 512:(half + 1) * 512], w_out[0:1024, half * 512:(half + 1) * 512], gate, [f"wbout{half}a"], chan=f"cv_out{half}a")
                dma("pool", wb_out[1024:1536, half * 512:(half + 1) * 512], w_out[1024:1536, half * 512:(half + 1) * 512], gate, [f"wbout{half}b"], chan=f"cv_out{half}b")
            for fq in range(4):
                for fs in (2 * fq, 2 * fq + 1):
                    dma("pool", wb_ff1[:, fs * 512:(fs + 1) * 512], w_ff1[:, fs * 512:(fs + 1) * 512], gate, [f"wbff1{fs}"], chan=f"cv_ff1{fs}")
                for half in range(2):
                    dma("pool", wb_ff2[fq * 1024:(fq + 1) * 1024, half * 512:(half + 1) * 512],
                        w_ff2[fq * 1024:(fq + 1) * 1024, half * 512:(half + 1) * 512], gate, [f"wbff2{fq}{half}"], chan=f"cv_ff2{fq}{half}")

        dma("sp", cst[:], consts[:, :], [], ["cst"], chan="setup", final=True)
        dma("sp", vec[:], vecs[:, :], [], ["vec"], chan="setup", final=True)
        dma("sp", rowb[:], rows[0, 0:1072].partition_broadcast(128), [], ["rowb"], chan="setup", final=True)
        convb_stage = ysb[0:1, :].rearrange("p (a b) -> p a b", a=1)
        dma("sp", ysb[0:1, :], rows[0:1, 1072:2096], [], ["ysb"], chan="setup", final=True)
        dma("sp", relu[0:1, 0, :], rows[0:1, 2096:2608], [], ["relu0"], chan="setup", final=True)

        mset("pool", small[:], 0.0, ["small", "ssq0", "ssq1", "ssq2", "ssq3", "rstd0", "rstd1", "rstd2", "rstd3", "gss", "grstd", "A_bc", "eacd", "w2", "rc", "dtr", "adiff", "eacd0", "eacd1", "w20", "w21"])
        mset("pool", epsc, EPS, ["small"])
        mset("pool", onec, 1.0, ["small"])
        mset("pool", ones_f[:], 1.0, ["ones_f"])
        mset("pool", ones_bf[:], 1.0, ["ones_bf"])
        conv_weights("in", [])
        mset("pool", Vaug[:], 1.0, ["Vaug"])
        mset("pool", qT2[:], 0.0, ["qT"])
        mset("pool", ovl[:], 0.0, ["uT", "xs0", "xs1", "xs2", "xs3"])
        mset("pool", Sst[:], 0.0, ["Sst"])
        mset("pool", Sbf[:], 0.0, ["Sbf"])
        cp("dve", ident_bf[:], ident_f, ["cst"], ["ident_bf"])
        cp("dve", Tm_bf[:], Tm_f, ["cst"], ["Tm_bf"])
        cp("dve", Um_bf[:], Um_f, ["cst"], ["Um_bf"])
        cp("dve", convb_bf[0:1, 0:1024], ysb[0:1, :], ["ysb"], ["convb_bf"])
        cp("dve", convb_bf[0:1, 1024:1536], relu[0:1, 0, :], ["relu0"], ["convb_bf"])
        act(A_bc, alog_bc, AF.Exp, ["rowb", "small"], ["A_bc"])
        tsc("dve", A_bc, A_bc, -1.0, None, ALU.mult, None, ["A_bc"], ["A_bc"])
        for c in range(12):
            for j in range(4):
                tsc("dve", diagW[:, c, j, :], ident_f, vec[:, 36 + c * 4 + j:37 + c * 4 + j], None, ALU.mult, None, ["cst", "vec"], ["diagW"])

        def setup_bias_tables():
            relu_flat = relu[:].rearrange("p a b -> p (a b)")
            for hh in range(8):
                if hh % 2 == 0:
                    stage, sres, ch = ysb[:, 0:640], "ysb", "bt0"
                else:
                    stage, sres, ch = relu_flat[:, 0:640], "relu0", "bt1"
                wr = [sres] if hh % 2 == 0 else ["relu0", "relu1"]
                dma("sp", stage, btab[:, hh * 640:(hh + 1) * 640], [], wr, chan=ch)
                act(expBT[:, hh].rearrange("p a b -> p (a b)"), stage, AF.Copy, wr, ["expBT"])
            mset("pool", expBT[64:128, :, 4, 0:64], -30000.0, ["expBT"])
            mset("pool", expBT[0:64, :, 0, 64:128], -30000.0, ["expBT"])

        def rms_to_T(t, from_stage, wcol, tagbanks):
            yield mset("dve", ssq, 0.0, ["ssq0", "ssq1", "ssq2", "ssq3"])
            for s in range(4):
                b = s % 2
                if from_stage:
                    r0 = (t * 4 + s) * 128
                    yield dma("sp", xst[:, b, :], x[r0:r0 + 128, :], [], [f"xst{b}", "hid"], chan=f"xst{b}")
                    src, sres = xst[:, b, :], f"xst{b}"
                else:
                    src, sres = h[:, s, :], f"h{s}"
                yield act(junk[:], src, AF.Square, [sres, f"ssq{s}"], ["junk", f"ssq{s}"], accum_out=ssq[:, s:s + 1])
                yield tsc("dve", rstd[:, s:s + 1], ssq[:, s:s + 1], 1.0 / D, EPS, ALU.mult, ALU.add, [f"ssq{s}"], [f"rstd{s}"])
                yield act(rstd[:, s:s + 1], rstd[:, s:s + 1], AF.Ln, [f"rstd{s}"], [f"rstd{s}"])
                yield act(rstd[:, s:s + 1], rstd[:, s:s + 1], AF.Exp, [f"rstd{s}"], [f"rstd{s}"], scale=-0.5)
                yield act(xsbf[:, b, :], src, AF.Copy, [sres, f"rstd{s}"], [f"xsbf{b}"], scale=rstd[:, s:s + 1])
                pb = tagbanks[b]
                pv = bank_bf(pb).rearrange("p (c n) -> p c n", c=8)
                yield tr_group([(pv[:, c, :], xsbf[:, b, c * 128:(c + 1) * 128]) for c in range(8)], [f"xsbf{b}", "ident_bf"], pres(pb))
                yield tt("dve", hnT[:, :, s * 128:(s + 1) * 128], pv, vec[:, wcol:wcol + 8].unsqueeze(2).to_broadcast([128, 8, 128]), ALU.mult,
                         pres(pb) + ["vec"], ["hnT"])

        rot = {"i": 0}

        def next_bank(lo=2, n=6):
            b = lo + rot["i"] % n
            rot["i"] += 1
            return b

        def in_proj(t):
            kcol = (t % 2) * 512
            for j in range(8):
                slot, sres = use_slot(("in", j))
                if j == 5 and t > 0:
                    yield cp("act", uT[:, :, 0:3], uhist[:], ["uhist"], ["uT"])
                if j in (0, 1, 5, 6, 7):
                    for cc in range(4):
                        b = next_bank()
                        yield mm_group([(bank(b), slot[:, k, cc * 128:(cc + 1) * 128], hnT[:, k, :], k == 0, k == 7) for k in range(8)],
                                       [sres, "hnT"], pres(b))
                        if j == 0:
                            yield act(qT2[0:64, 0, cc, :], bank(b)[0:64, :], AF.Copy, pres(b), ["qT"], scale=0.125)
                            yield act(qT2[64:128, 1, cc, :], bank(b)[64:128, :], AF.Copy, pres(b), ["qT"], scale=0.125)
                        elif j == 1:
                            yield cp("dve", kT[:, cc, kcol:kcol + 512], bank(b), pres(b), ["kT"])
                        else:
                            c = (j - 5) * 4 + cc
                            yield cp("act", uT[:, c, 3:515], bank(b), pres(b), ["uT"])
                else:
                    for s in range(4):
                        b = next_bank()
                        yield mm_group([(bank(b), hnT[:, k, s * 128:(s + 1) * 128], slot[:, k, :], k == 0, k == 7) for k in range(8)],
                                       [sres, "hnT"], pres(b))
                        if j == 2:
                            blk = (t * 4 + s) % 8
                            yield cp("dve", Vaug[:, blk, :, 0:64], bank(b).rearrange("p (a d) -> p a d", a=8), pres(b), ["Vaug"])
                        else:
                            yield act(zs[:, s, (j - 3) * 512:(j - 2) * 512], bank(b), AF.Silu, pres(b), ["zs"])
            yield cp("act", uhist[:], uT[:, :, 512:515], ["uT"], ["uhist"])
            b = next_bank()
            for s in range(4):
                yield mm_group([(bank(b)[:, s * 16:(s + 1) * 16], hnT[:, k, s * 128:(s + 1) * 128], wdt[:, k, :], k == 0, k == 7) for k in range(8)],
                               ["wdt", "hnT"], pres(b))
            yield tt("dve", dtr, bank(b)[:, 0:64].rearrange("p (s n) -> p s n", s=4), dtb_bc.unsqueeze(1).to_broadcast([128, 4, 16]), ALU.add,
                     pres(b) + ["rowb"], ["dtr"])
            yield act(dtr, dtr, AF.Exp, ["dtr"], ["dtr"])
            yield act(dtt[:], dtr, AF.Ln, ["dtr", "small"], ["dtt"], bias=onec)
            yield tt("dve", aa[:], dtt[:], A_bc.unsqueeze(1).to_broadcast([128, 4, 16]), ALU.mult, ["dtt", "A_bc"], ["aa"])

        def conv_fm(t):
            for ci, c in enumerate(range(8, 12)):
                b = next_bank()
                yield mm_group([(bank(b), diagW[:, c, j, :], uT[:, c, j:j + 512], j == 0, j == 3) for j in range(4)], ["diagW", "uT"], pres(b))
                yield act(BCT[:, ci, :], bank(b), AF.Silu, pres(b) + ["vec"], ["BCT"], bias=vec[:, 24 + c:25 + c])

        prog = {}

        def conv_tm_all(t):
            for s_ in range(4):
                xres = f"xs{s_}"
                for g, (c0, c1) in enumerate(((0, 4), (4, 8), (8, 10))):
                    b_ = next_bank()
                    items = []
                    for c in range(c0, c1):
                        reg = bank(b_)[:, (c - c0) * 128:(c - c0 + 1) * 128]
                        items.append((reg, ones_bf[0:1, :], convb_bf[0:1, c * 128:(c + 1) * 128], True, False))
                        for j in range(4):
                            items.append((reg, uT[:, c, s_ * 128 + j:s_ * 128 + j + 128], diagW[:, c, j, :], False, j == 3))
                    yield mm_group(items, ["ones_bf", "convb_bf", "uT", "diagW"], pres(b_))
                    n = (c1 - c0) * 128
                    yield act(xs_tm[:, s_, c0 * 128:c0 * 128 + n], bank(b_)[:, 0:n], AF.Silu, pres(b_), [xres])

        def ssd_pre(t):
            b_ = next_bank()
            items = []
            for s_ in range(4):
                items.append((bank(b_)[:, s_ * 32:s_ * 32 + 16], Tm_f, aa[:, s_, :], True, True))
                items.append((bank(b_)[:, s_ * 32 + 16:s_ * 32 + 32], ones_f[:], aa[:, s_, :], True, True))
            yield mm_group(items, ["cst", "ones_f", "aa"], pres(b_))
            pv4 = bank(b_)[:, 0:128].rearrange("p (s n) -> p s n", s=4)
            yield cp("act", raw4[:], pv4, pres(b_), ["raw4"])
            yield tt("dve", w24[:], raw4[:, :, 16:32], raw4[:, :, 0:16], ALU.subtract, ["raw4"], ["w24"])
            yield act(eacd4[:], raw4[:], AF.Exp, ["raw4"], ["eacd4"])
            yield act(w24[:], w24[:], AF.Exp, ["w24"], ["w24"])
            yield tt("dve", w24[:], w24[:], dtt[:], ALU.mult, ["w24", "dtt"], ["w24"])
            yield cp("dve", ahl4[:, 0], aa[:], ["aa"], ["ahl4"])
            yield tt("dve", ahl4[:, 1], aa[:], ahl4[:, 0], ALU.subtract, ["aa", "ahl4"], ["ahl4"])

        def ssd_front(t, s):
            xb = s
            xsx3 = xs_tm[:, xb, 0:1024].rearrange("p (a d) -> p a d", a=16)
            xres = f"xs{xb}"
            cs = s * 128
            for r in range(2):
                for hl in range(2):
                    yield tt("dve", AU[:, hl], Um_bf[:].unsqueeze(1).to_broadcast([128, 8, 128]),
                             ahl4[:, hl, s, r * 8:(r + 1) * 8].unsqueeze(2).to_broadcast([128, 8, 128]), ALU.mult, ["Um_bf", "ahl4"], [f"AU{hl}"])
                for q in range(2):
                    bq = 1
                    items = []
                    for i in range(4):
                        reg = bank(bq)[:, i * 128:(i + 1) * 128]
                        items.append((reg, AU[:, 0, q * 4 + i, :], Tm_bf[:], True, False))
                        items.append((reg, AU[:, 1, q * 4 + i, :], Tm_bf[:], False, True))
                    yield mm_group(items, ["AU0", "AU1", "Tm_bf"], pres(bq))
                    yield act(DT[:, r * 8 + q * 4:r * 8 + q * 4 + 4, :], bank(bq).rearrange("p (a n) -> p a n", a=4), AF.Exp, pres(bq), ["DT"])
            yield mm_group([(bank(1)[:, g * 128:128 + g * 128], BCT[:, g, cs:cs + 128], BCT[:, 2 + g, cs:cs + 128], True, True) for g in range(2)],
                           ["BCT"], ["ps1"])
            yield tt("dve", CBm[:], bank(1)[:, 0:256].rearrange("p (g n) -> p g n", g=2), Tm_f.unsqueeze(1).to_broadcast([128, 2, 128]), ALU.mult,
                     ["ps1", "cst"], ["CBm"])
            while prog.get(("By", t), 0) < s:
                yield WAIT
            yield tt("dve", MT[:].rearrange("p (g a) n -> p g a n", g=2), DT[:].rearrange("p (g a) n -> p g a n", g=2),
                     CBm[:].unsqueeze(2).to_broadcast([128, 2, 8, 128]), ALU.mult, ["DT", "CBm"], ["MT"])
            yield tt("pool", Xdt[:].rearrange("p (a d) -> p a d", a=16), xsx3, dtt[:, s, :].unsqueeze(2).to_broadcast([128, 16, 64]), ALU.mult,
                     [xres, "dtt"], ["Xdt"])
            yield tt("pool", Xsk[:].rearrange("p (a d) -> p a d", a=16), xsx3, dsk_bc.unsqueeze(2).to_broadcast([128, 16, 64]), ALU.mult,
                     [xres, "rowb"], ["Xsk"])
            prog[("F", t)] = s + 1

        def ssd_tail(t, s):
            cs = s * 128
            pv = bank_bf(2).rearrange("p (c n) -> p c n", c=8)
            yield tr_group([(pv[:, c, :], yn[:, c * 128:(c + 1) * 128]) for c in range(8)], ["yn", "ident_bf"], pres(2))
            yield tt("dve", mixT[:, 4:12, cs:cs + 128], pv, vec[:, 8:16].unsqueeze(2).to_broadcast([128, 8, 128]), ALU.mult, pres(2) + ["vec"], ["mixT"])

        def ssd_back(t, s):
            xb = s
            xsx3 = xs_tm[:, xb, 0:1024].rearrange("p (a d) -> p a d", a=16)
            xres = f"xs{xb}"
            cs = s * 128
            ea_cd = eacd4[:, s, :]
            w2b = w24[:, s, :]
            yield tt("pool", Xd[:].rearrange("p (a d) -> p a d", a=16), xsx3, w2b.unsqueeze(2).to_broadcast([128, 16, 64]), ALU.mult,
                     [xres, "w24"], ["Xd"])
            for g in range(2):
                bg = 2 if g == 0 else 0
                yield mm_group([(bank(bg), BCT[:, 2 + g, cs:cs + 128], Sbf[:, g * 512:(g + 1) * 512], True, True)], ["BCT", "Sbf"], pres(bg))
            for g in range(2):
                bg = 2 if g == 0 else 0
                ys = ysb[:, g * 512:(g + 1) * 512]
                yield tt("dve", ys.rearrange("p (a d) -> p a d", a=8), bank(bg).rearrange("p (a d) -> p a d", a=8),
                         ea_cd[:, g * 8:(g + 1) * 8].unsqueeze(2).to_broadcast([128, 8, 64]), ALU.mult, pres(bg) + ["eacd4"], ["ysb"])
            while prog.get(("F", t), 0) < s + 1:
                yield WAIT
            for g in range(2):
                bg = 2 if g == 0 else 0
                items = [(bank(bg), ident_bf[:], Xsk[:, g * 512:(g + 1) * 512], True, False)]
                for hh in range(8):
                    hd = g * 8 + hh
                    items.append((bank(bg)[:, hh * 64:(hh + 1) * 64], MT[:, hd, :], Xdt[:, hd * 64:(hd + 1) * 64], False, True))
                yield mm_group(items, ["ident_bf", "Xsk", "MT", "Xdt"], pres(bg))
            for g in range(2):
                bg = 2 if g == 0 else 0
                ys = ysb[:, g * 512:(g + 1) * 512]
                yield tt("dve", ys, ys, bank(bg), ALU.add, ["ysb"] + pres(bg), ["ysb"])
            prog[("By", t)] = s + 1
            if s > 0:
                yield from ssd_tail(t, s - 1)
            yield tt("dve", Sst[:].rearrange("p (a d) -> p a d", a=16), Sst[:].rearrange("p (a d) -> p a d", a=16),
                     ea_cd[:, 16:32].unsqueeze(2).to_broadcast([128, 16, 64]), ALU.mult, ["Sst", "eacd4"], ["Sst"])
            for g in range(2):
                bg = 2 if g == 0 else 0
                yield mm_group([(bank(bg), xs_tm[:, xb, 1024 + g * 128:1152 + g * 128], Xd[:, g * 512:(g + 1) * 512], True, True)], [xres, "Xd"], pres(bg))
            for g in range(2):
                bg = 2 if g == 0 else 0
                yield tt("dve", Sst[:, g * 512:(g + 1) * 512], Sst[:, g * 512:(g + 1) * 512], bank(bg), ALU.add, ["Sst"] + pres(bg), ["Sst"])
            yield cp("act", Sbf[:], Sst[:], ["Sst"], ["Sbf"])
            yield tt("dve", ysb[:], ysb[:], zs[:, s, :], ALU.mult, ["ysb", "zs"], ["ysb"])
            yield mset("dve", gss, 0.0, ["gss"])
            for g in range(2):
                yield act(junk[:, 0:512], ysb[:, g * 512:(g + 1) * 512], AF.Square, ["ysb", "gss"], ["junk", "gss"], accum_out=gss[:, g:g + 1])
            yield tsc("dve", grstd, gss, 1.0 / 512, EPS, ALU.mult, ALU.add, ["gss"], ["grstd"])
            yield act(grstd, grstd, AF.Ln, ["grstd"], ["grstd"])
            yield act(grstd, grstd, AF.Exp, ["grstd"], ["grstd"], scale=-0.5)
            yield tt("dve", yn[:].rearrange("p (g n) -> p g n", g=2), ysb[:].rearrange("p (g n) -> p g n", g=2),
                     grstd.unsqueeze(2).to_broadcast([128, 2, 512]), ALU.mult, ["ysb", "grstd"], ["yn"])
            if s == 3:
                yield from ssd_tail(t, 3)
            prog[("Bdone", t)] = s + 1

        def ssd_F(t):
            for s in range(4):
                yield from ssd_front(t, s)

        def ssd_B(t):
            for s in range(4):
                yield from ssd_back(t, s)

        def att_tail(m):
            pv = bank_bf(3)[:, 0:512].rearrange("p (c n) -> p c n", c=4)
            yield tr_group([(pv[:, c, :], att_tm[:, m % 2, c * 128:(c + 1) * 128]) for c in range(4)], [f"att_tm{m % 2}", "ident_bf"], pres(3))
            yield cp("dve", mixT[:, 0:4, m * 128:(m + 1) * 128], pv, pres(3), ["mixT"])

        def attention(t):
            for m in range(4):
                gm = t * 4 + m
                nv = min(gm, 4) + 1
                gb0 = gm - (nv - 1)
                for hp in range(4):
                    SC = PS[:, 3 * 512:3 * 512 + 1280].rearrange("p (a j n) -> p a j n", a=2, j=5)
                    items = []
                    for hd in range(2):
                        r0 = hd * 64
                        for jj in range(nv):
                            kc = ((gb0 + jj) % 8) * 128
                            items.append((SC[:, hd, jj, :], kT[:, hp, kc:kc + 128], qT2[:, hd, hp, m * 128:(m + 1) * 128], True, False))
                            items.append((SC[:, hd, jj, :], ident_bf[:], expBT[:, 2 * hp + hd, 5 - nv + jj, :], False, True))
                    sres = ["ps3", "ps4", "ps5"]
                    yield mm_group(items, ["kT", "qT", "ident_bf", "expBT"], sres)
                    Ev = E[:, :, 0:nv, :]
                    yield act(Ev, SC[:, :, 0:nv, :], AF.Exp, sres, ["E"])
                    PO = bank(5)[:, 256:386].rearrange("p (a d) -> p a d", a=2)
                    items = []
                    for hd in range(2):
                        for jj in range(nv):
                            items.append((PO[:, hd, :], E[:, hd, jj, :], Vaug[:, (gb0 + jj) % 8, 2 * hp + hd, :], jj == 0, jj == nv - 1))
                    yield mm_group(items, ["E", "Vaug"], ["ps5"])
                    yield R.op("dve", lambda e, PO=PO: e.reciprocal(out=rc, in_=PO[:, :, 64]), ["ps5"], ["rc"])
                    yield tt("dve", att_tm[:, m % 2, hp * 128:(hp + 1) * 128].rearrange("p (a d) -> p a d", a=2), PO[:, :, 0:64],
                             rc.unsqueeze(2).to_broadcast([128, 2, 64]), ALU.mult, ["ps5", "rc"], [f"att_tm{m % 2}"])
                    if hp == 1 and m > 0:
                        yield from att_tail(m - 1)
                if m == 3:
                    yield from att_tail(3)

        def load_h(t):
            for s in range(4):
                r0 = (t * 4 + s) * 128
                yield dma("sp", h[:, s, :], x[r0:r0 + 128, :], [], [f"h{s}"], chan=f"x{s}")

        def out_proj(t):
            for half in range(2):
                slotA, ra = use_slot(("out", half, 0))
                for s in range(4):
                    yield mm_group([(bank(s), mixT[:, k, s * 128:(s + 1) * 128], slotA[:, k, :], k == 0, False) for k in range(8)], [ra, "mixT"], pres(s))
                slotB, rb = use_slot(("out", half, 1))
                for s in range(4):
                    yield mm_group([(bank(s), mixT[:, 8 + k, s * 128:(s + 1) * 128], slotB[:, k, :], False, k == 3) for k in range(4)], [rb, "mixT"], pres(s))
                    hs = h[:, s, half * 512:(half + 1) * 512]
                    yield tt("dve", hs, hs, bank(s), ALU.add, [f"h{s}"] + pres(s), [f"h{s}"])

        def ffn(t):
            yield from rms_to_T(t, False, 16, (6, 7))
            i = 0
            for fq in range(4):
                for fsi in range(2):
                    slot, sres = use_slot(("ff1", fq * 2 + fsi))
                    for fc in range(4):
                        b = 6 + i % 2
                        rb = i % 2
                        i += 1
                        yield mm_group([(bank(b), slot[:, k, fc * 128:(fc + 1) * 128], hnT[:, k, :], k == 0, k == 7) for k in range(8)], [sres, "hnT"], pres(b))
                        yield act(relu[:, rb, :], bank(b), AF.Relu, pres(b), [f"relu{rb}"])
                        yield tt("pool", hidT[:, fsi * 4 + fc, :], relu[:, rb, :], relu[:, rb, :], ALU.mult, [f"relu{rb}"], ["hid", "xst0", "xst1"])
                for half in range(2):
                    slot, sres = use_slot(("ff2", fq, half))
                    for s in range(4):
                        b = 6 + i % 2
                        i += 1
                        rb = i % 2
                        yield mm_group([(bank(b), hidT[:, k, s * 128:(s + 1) * 128], slot[:, k, :], k == 0, k == 7) for k in range(8)], [sres, "hid"], pres(b))
                        hs = h[:, s, half * 512:(half + 1) * 512]
                        if FFN2_VIA_POOL:
                            yield cp("act", relu[:, rb, :], bank(b), pres(b), [f"relu{rb}"])
                            yield tt("pool", hs, hs, relu[:, rb, :], ALU.add, [f"h{s}", f"relu{rb}"], [f"h{s}"])
                        else:
                            yield tt("dve", hs, hs, bank(b), ALU.add, [f"h{s}"] + pres(b), [f"h{s}"])
            yield from final_norm_store(t)

        lastout = {}

        def final_norm_store(t):
            yield mset("dve", ssq, 0.0, ["ssq0", "ssq1", "ssq2", "ssq3"])
            for s in range(4):
                yield act(junk[:], h[:, s, :], AF.Square, [f"h{s}", f"ssq{s}"], ["junk", f"ssq{s}"], accum_out=ssq[:, s:s + 1])
                yield tsc("dve", rstd[:, s:s + 1], ssq[:, s:s + 1], 1.0 / D, EPS, ALU.mult, ALU.add, [f"ssq{s}"], [f"rstd{s}"])
                yield act(rstd[:, s:s + 1], rstd[:, s:s + 1], AF.Ln, [f"rstd{s}"], [f"rstd{s}"])
                yield act(rstd[:, s:s + 1], rstd[:, s:s + 1], AF.Exp, [f"rstd{s}"], [f"rstd{s}"], scale=-0.5)
                yield R.op("dve", lambda e, s=s: e.scalar_tensor_tensor(out=h[:, s, :], in0=h[:, s, :], scalar=rstd[:, s:s + 1], in1=nfw_bc, op0=ALU.mult, op1=ALU.mult),
                           [f"h{s}", f"rstd{s}", "rowb"], [f"h{s}"])
                r0 = (t * 4 + s) * 128
                lastout[s] = dma("sp", y[r0:r0 + 128, :], h[:, s, :], [f"h{s}"], [], chan=f"out{s}")
                yield lastout[s]

        def dump(name, ap, shape, res):
            o = dout(name, shape)
            dma("pool", o, ap, res, [], chan="dbg", final=True)

        rms_done = set()

        def ffn_plus(t):
            yield from ffn(t)
            if t + 2 < NT:
                rms_done.add(t + 2)
                yield from rms_to_T(t + 2, True, 0, (6, 7))

        def phaseA(t):
            if t not in rms_done:
                yield from rms_to_T(t, True, 0, (0, 1))
            yield from in_proj(t)
            yield from conv_fm(t)
            yield from conv_tm_all(t)
            yield from ssd_pre(t)

        run(rms_to_T(0, True, 0, (0, 1)))
        run(in_proj(0))
        setup_bias_tables()
        run(conv_fm(0))
        run(conv_tm_all(0))
        run(ssd_pre(0))
        conv_weights("rest", ["BCT"])
        wst["rest_ready"] = True
        interleave(("B", 0), {"ssdF": ssd_F(0), "ssdB": ssd_B(0), "att": attention(0)})
        for t in range(NT):
            if t + 1 < NT:
                run(phaseA(t + 1))
            run(load_h(t))
            if debug and t == debug - 1:
                dump("d_mixT", mixT[:], [128, 12, 512], ["mixT"])
            run(out_proj(t))
            if debug and t == debug - 1:
                dump("d_h1", h[:], [128, 4, 1024], ["h0", "h1", "h2", "h3"])
            if t + 1 < NT:
                if SEQ_MODE == 1:
                    run(ffn(t)); interleave(("C", t), {"ssdF": ssd_F(t + 1), "ssdB": ssd_B(t + 1)}); run(attention(t + 1))
                elif SEQ_MODE == 2:
                    interleave(("C", t), {"ffn": ffn(t), "att": attention(t + 1)}); run(ssd_all(t + 1))
                elif SEQ_MODE == 3:
                    interleave(("C", t), {"ffn": ffn(t), "ssd": ssd_all(t + 1)}); run(attention(t + 1))
                else:
                    interleave(("C", t), {"ffn": ffn_plus(t), "ssdF": ssd_F(t + 1), "ssdB": ssd_B(t + 1), "att": attention(t + 1)})
            else:
                run(ffn(t))

        fin = R.op("sp", lambda e: e.nop(), [], [])
        for o in lastout.values():
            fin.deps.add(o)
        for e in ENGS:
            for o in R.ops[e]:
                if o.chan == "dbg":
                    fin.deps.add(o)
        R.emit_all(nc, es)
    return nc, dbg, plan_out


def host_layout(inputs, NT=8):
    f = lambda a: np.ascontiguousarray(np.asarray(a, dtype=np.float32))
    S = NT * 512
    w_in = f(inputs["w_in"][0])
    w_out = f(inputs["w_out"][0])
    w_ff1 = f(inputs["w_ff1"][0])
    w_ff2 = f(inputs["w_ff2"][0])
    fm = lambda v: f(v).reshape(-1, 128).T
    conv_w = f(inputs["conv_w"][0])
    convw_fm = conv_w.reshape(4, 12, 128).transpose(2, 1, 0).reshape(128, 48)
    vecs = np.concatenate([fm(inputs["norm_mix_w"][0]), fm(inputs["ssd_norm_w"][0]), fm(inputs["norm_mlp_w"][0]),
                           fm(inputs["conv_b"][0]), convw_fm], axis=1)
    rows = np.concatenate([f(inputs["norm_final_w"]), f(inputs["dt_bias"][0]), f(inputs["a_log"][0]), f(inputs["d_skip"][0]),
                           f(inputs["conv_b"][0])])[None, :]
    rb = f(inputs["rel_bias"][0])
    ki = np.arange(128)[:, None, None]
    jj = np.arange(5)[None, :, None]
    qi = np.arange(128)[None, None, :]
    idx = np.clip(128 * (4 - jj) + qi - ki, -256, 256) + 256
    btab = rb[idx]
    btab = np.ascontiguousarray(btab.transpose(0, 3, 1, 2)).reshape(128, 8 * 640)
    k = np.arange(128)
    ident = np.eye(128, dtype=np.float32)
    Tm = (k[:, None] <= k[None, :]).astype(np.float32)
    Um = (k[:, None] > k[None, :]).astype(np.float32)
    consts = np.concatenate([ident, Tm, Um], axis=1)
    common = {"w_in": w_in, "w_out": w_out, "w_ff1": w_ff1, "w_ff2": w_ff2, "vecs": f(vecs), "rows": f(rows),
              "btab": f(btab), "consts": f(consts)}
    xs = f(inputs["x"])
    return [dict(common, x=np.ascontiguousarray(xs[b, :S])) for b in range(xs.shape[0])]


_CACHE = {}


def kernel(**inputs):
    NT = 8
    in_maps = host_layout(inputs, NT)
    if "nc" not in _CACHE:
        _CACHE["nc"] = build_program(NT)[0]
    nc = _CACHE["nc"]
    res = run_bass_kernel_spmd(nc, in_maps, core_ids=list(range(8)))
    out = np.stack([np.asarray(r["y"]) for r in res.results], axis=0)
    return out.astype(np.float32)
```

```python
import numpy as np
from contextlib import ExitStack
import concourse.bass as bass
import concourse.mybir as mybir
from concourse.bass_utils import run_bass_kernel_spmd

F32 = mybir.dt.float32
BF16 = mybir.dt.bfloat16
AF = mybir.ActivationFunctionType
ALU = mybir.AluOpType

D = 1024
INP = 4112
EPS = 1e-5
NSLOT = 3
import os
FFN2_VIA_POOL = int(os.environ.get('FFN2_VIA_POOL', '0'))
LEAD = float(os.environ.get('LEAD', '1.15'))
NOTILE = int(os.environ.get('NOTILE', '0'))
SEQ_MODE = int(os.environ.get('SEQ_MODE', '0'))
ENGS = ("pe", "act", "dve", "pool", "sp")


class Op:
    __slots__ = ("eng", "emit", "deps", "chan", "val", "milestone", "ms", "final")

    def __init__(self, eng, emit, chan):
        self.eng, self.emit, self.chan = eng, emit, chan
        self.deps = set()
        self.val = 0
        self.milestone = False
        self.ms = 0
        self.final = False


class Rec:
    def __init__(self):
        self.ops = {e: [] for e in ENGS}
        self.lastw = {}
        self.readers = {}
        self.chan = {}
        self.final_chans = set()

    def op(self, eng, emit, reads=(), writes=(), chan=None, final=False):
        o = Op(eng, emit, chan)
        deps = set()
        for r in reads:
            w = self.lastw.get(r)
            if w is not None:
                deps.add(w)
        for w_ in writes:
            w = self.lastw.get(w_)
            if w is not None:
                deps.add(w)
            for rd in self.readers.get(w_, {}).values():
                deps.add(rd)
        if eng == "pe":
            deps = {d for d in deps if not (d.eng == "pe" and d.chan is None)}
        o.deps = deps
        key = eng if chan is None else ("dma", chan)
        for r in reads:
            self.readers.setdefault(r, {})[key] = o
        for w_ in writes:
            self.lastw[w_] = o
            self.readers[w_] = {}
        if chan is not None:
            n = self.chan.get(chan, 0) + 1
            self.chan[chan] = n
            o.val = 16 * n
            o.final = final
            if final:
                self.final_chans.add(chan)
        self.ops[eng].append(o)
        return o

    def emit_all(self, nc, es):
        for e in ENGS:
            for o in self.ops[e]:
                for d in o.deps:
                    d.milestone = True
        engsem = {e: es.enter_context(nc.semaphore("sem_" + e)) for e in ENGS}
        chansem = {c: es.enter_context(nc.semaphore("ch_" + str(c))) for c in self.chan}
        for e in ENGS:
            k = 0
            for o in self.ops[e]:
                if o.chan is None and o.milestone:
                    k += 1
                    o.ms = k
        block = es.enter_context(nc.Block())
        rec = self

        def run(e, eng):
            waited = {}
            for o in rec.ops[e]:
                for d in o.deps:
                    if d.chan is not None:
                        sem = chansem[d.chan]
                        val = 16 * rec.chan[d.chan] if d.final else d.val
                    else:
                        sem = engsem[d.eng]
                        val = d.ms
                    kk = sem.num
                    if waited.get(kk, 0) < val:
                        eng.wait_ge(sem, val)
                        waited[kk] = val
                ins = o.emit(eng)
                if o.chan is not None:
                    ins.then_inc(chansem[o.chan], 16)
                elif o.milestone:
                    ins.then_inc(engsem[e], 1)

        @block.sync
        def _(eng):
            run("sp", eng)

        @block.tensor
        def _(eng):
            run("pe", eng)

        @block.scalar
        def _(eng):
            run("act", eng)

        @block.vector
        def _(eng):
            run("dve", eng)

        @block.gpsimd
        def _(eng):
            run("pool", eng)


def build_program(NT=8, debug=False):
    _, _, plan = _build(NT, debug, None)
    nc, dbg, _ = _build(NT, debug, plan)
    return nc, dbg


def _build(NT, debug, plan):
    S = NT * 512
    nc = bass.Bass("TRN2", target_bir_lowering=False)
    R = Rec()
    plan_out = {"seq": [], "counts": {}}

    def din(name, shape, dt=F32):
        return nc.dram_tensor(name, list(shape), dt, kind="ExternalInput").ap()

    x = din("x", [S, D])
    w_in = din("w_in", [D, INP])
    w_out = din("w_out", [1536, D])
    w_ff1 = din("w_ff1", [D, 4096])
    w_ff2 = din("w_ff2", [4096, D])
    vecs = din("vecs", [128, 84])
    rows = din("rows", [1, 2608])
    btab = din("btab", [128, 8 * 640])
    consts = din("consts", [128, 384])
    y = nc.dram_tensor("y", [S, D], F32, kind="ExternalOutput").ap()
    wb_in = nc.dram_tensor("wb_in", [D, 4096], BF16, kind="Internal").ap()
    wb_out = nc.dram_tensor("wb_out", [1536, D], BF16, kind="Internal").ap()
    wb_ff1 = nc.dram_tensor("wb_ff1", [D, 4096], BF16, kind="Internal").ap()
    wb_ff2 = nc.dram_tensor("wb_ff2", [4096, D], BF16, kind="Internal").ap()
    dbg = {}

    def dout(name, shape):
        a = nc.dram_tensor(name, list(shape), F32, kind="ExternalOutput").ap()
        dbg[name] = a
        return a

    es = ExitStack()
    with es:
        def sb(name, shape, dt):
            return es.enter_context(nc.sbuf_tensor(name, list(shape), dt))

        ring = sb("ring", [128, NSLOT, 8, 512], BF16)
        h = sb("h", [128, 4, 1024], F32)
        hnT = sb("hnT", [128, 8, 512], BF16)
        qT2 = sb("qT2", [128, 2, 4, 512], BF16)
        kT = sb("kT", [128, 4, 1024], BF16)
        Vaug = sb("Vaug", [128, 8, 8, 65], BF16)
        zs = sb("zs", [128, 4, 1024], BF16)
        ovl = sb("ovl", [128, 11312], BF16)
        uT = ovl[:, 0:6192].rearrange("p (c n) -> p c n", c=12)
        xs_tm = ovl[:, 6192:11312].rearrange("p (b n) -> p b n", b=4)
        hidT = sb("hidT", [128, 8, 512], BF16)
        xst = hidT[:].rearrange("p a b -> p (a b)").bitcast(F32).rearrange("p (b n) -> p b n", b=2)
        BCT = sb("BCT", [128, 4, 512], BF16)
        AU = sb("AU", [128, 2, 8, 128], BF16)
        DT = sb("DT", [128, 16, 128], BF16)
        MT = sb("MT", [128, 16, 128], BF16)
        CBm = sb("CBm", [128, 2, 128], BF16)
        Xdt = sb("Xdt", [128, 1024], BF16)
        Xd = sb("Xd", [128, 1024], BF16)
        Sst = sb("Sst", [128, 1024], F32)
        Sbf = sb("Sbf", [128, 1024], BF16)
        ysb = sb("ysb", [128, 1024], F32)
        yn = sb("yn", [128, 1024], BF16)
        Xsk = sb("Xsk", [128, 1024], BF16)
        E = sb("E", [128, 2, 5, 128], BF16)
        att_tm = sb("att_tm", [128, 2, 512], BF16)
        mixT = sb("mixT", [128, 12, 512], BF16)
        relu = sb("relu", [128, 2, 512], F32)
        expBT = sb("expBT", [128, 8, 5, 128], BF16)
        diagW = sb("diagW", [128, 12, 4, 128], BF16)
        cst = sb("cst", [128, 384], F32)
        vec = sb("vec", [128, 84], F32)
        rowb = sb("rowb", [128, 1072], F32)
        convb_bf = sb("convb_bf", [1, 1536], BF16)
        ones_bf = sb("ones_bf", [1, 128], BF16)
        ones_f = sb("ones_f", [128, 128], F32)
        ident_bf = sb("ident_bf", [128, 128], BF16)
        Tm_bf = sb("Tm_bf", [128, 128], BF16)
        Um_bf = sb("Um_bf", [128, 128], BF16)
        wdt = sb("wdt", [128, 8, 16], BF16)
        junk = sb("junk", [128, 1024], BF16)
        xsbf = sb("xsbf", [128, 2, 1024], BF16)
        small = sb("small", [128, 256], F32)
        dtt = sb("dtt", [128, 4, 16], F32)
        aa = sb("aa", [128, 4, 16], F32)
        ahl = sb("ahl", [128, 2, 16], BF16)
        uhist = sb("uhist", [128, 12, 3], BF16)
        eacd4 = sb("eacd4", [128, 4, 32], F32)
        w24 = sb("w24", [128, 4, 16], F32)
        raw4 = sb("raw4", [128, 4, 32], F32)
        ahl4 = sb("ahl4", [128, 2, 4, 16], BF16)
        PS = es.enter_context(nc.psum_tensor("PS", [128, 4096], F32))

        ident_f = cst[:, 0:128]
        Tm_f = cst[:, 128:256]
        Um_f = cst[:, 256:384]
        nfw_bc = rowb[:, 0:1024]
        dtb_bc = rowb[:, 1024:1040]
        alog_bc = rowb[:, 1040:1056]
        dsk_bc = rowb[:, 1056:1072]
        ssq = small[:, 0:4]
        rstd = small[:, 4:8]
        gss = small[:, 8:10]
        grstd = small[:, 10:12]
        epsc = small[:, 12:13]
        onec = small[:, 13:14]
        A_bc = small[:, 16:32]
        eacd = small[:, 32:64]
        w2 = small[:, 64:80]
        rc = small[:, 80:82]
        adiff = small[:, 160:176]
        eacd2 = small[:, 176:240].rearrange("p (b n) -> p b n", b=2)
        w22 = small[:, 32:64].rearrange("p (b n) -> p b n", b=2)
        dtr = small[:, 96:160].rearrange("p (s n) -> p s n", s=4)

        def bank(b):
            return PS[:, b * 512:(b + 1) * 512]

        def pres(b):
            return ["ps%d" % b]

        def bank_bf(b):
            return PS[:, b * 512:(b + 1) * 512].bitcast(BF16)

        def mm_group(items, reads, writes):
            def emit(e):
                ins = None
                for (o, l, r, st, sp) in items:
                    ins = e.matmul(o, lhsT=l, rhs=r, start=st, stop=sp, skip_group_check=True)
                return ins
            return R.op("pe", emit, reads, writes)

        def tr_group(items, reads, writes):
            def emit(e):
                ins = None
                for (o, i) in items:
                    ins = e.transpose(out=o, in_=i, identity=ident_bf[:])
                return ins
            return R.op("pe", emit, reads, writes)

        def act(out, in_, func, reads, writes, **kw):
            return R.op("act", lambda e: e.activation(out=out, in_=in_, func=func, **kw), reads, writes)

        def tt(eng, out, in0, in1, op, reads, writes):
            return R.op(eng, lambda e: e.tensor_tensor(out=out, in0=in0, in1=in1, op=op), reads, writes)

        def tsc(eng, out, in0, s1, s2, op0, op1, reads, writes):
            if s2 is None:
                return R.op(eng, lambda e: e.tensor_scalar(out=out, in0=in0, scalar1=s1, scalar2=None, op0=op0), reads, writes)
            return R.op(eng, lambda e: e.tensor_scalar(out=out, in0=in0, scalar1=s1, scalar2=s2, op0=op0, op1=op1), reads, writes)

        def cp(eng, out, in_, reads, writes):
            if eng == "act":
                return R.op(eng, lambda e: e.activation(out=out, in_=in_, func=AF.Copy), reads, writes)
            return R.op(eng, lambda e: e.tensor_copy(out=out, in_=in_), reads, writes)

        def mset(eng, ap, val, writes):
            return R.op(eng, lambda e: e.memset(ap, val), (), writes)

        def dma(eng, out, in_, reads, writes, chan, final=False):
            return R.op(eng, lambda e: e.dma_start(out=out, in_=in_), reads, writes, chan=chan, final=final)

        def run(gen):
            for _ in gen:
                pass

        WAIT = "WAIT"

        def interleave(key, streams):
            names = list(streams)
            cnt = {n: 0 for n in names}
            alive = dict(streams)
            tot = plan["counts"].get(key) if plan else None
            while alive:
                if tot:
                    order = sorted(alive, key=lambda k: (cnt[k] + 0.5) / max(tot[k] * (1.0 if k == "ffn" else LEAD), 1))
                else:
                    order = sorted(alive, key=lambda k: cnt[k])
                progressed = False
                for n in order:
                    try:
                        r = next(alive[n])
                    except StopIteration:
                        del alive[n]
                        progressed = True
                        break
                    if r is WAIT:
                        continue
                    cnt[n] += 1
                    progressed = True
                    break
                assert progressed, ("interleave deadlock", key, list(alive))
            plan_out["counts"][key] = cnt

        def wsrc(key):
            kind = key[0]
            if kind == "in":
                j = key[1]
                return wb_in[:, j * 512:(j + 1) * 512].rearrange("(c p) n -> p c n", p=128), 8, f"wbin{j}"
            if kind == "out":
                half, part = key[1], key[2]
                if part == 0:
                    return wb_out[0:1024, half * 512:(half + 1) * 512].rearrange("(c p) n -> p c n", p=128), 8, f"wbout{half}a"
                return wb_out[1024:1536, half * 512:(half + 1) * 512].rearrange("(c p) n -> p c n", p=128), 4, f"wbout{half}b"
            if kind == "ff1":
                fs = key[1]
                return wb_ff1[:, fs * 512:(fs + 1) * 512].rearrange("(c p) n -> p c n", p=128), 8, f"wbff1{fs}"
            fg, half = key[1], key[2]
            return wb_ff2[fg * 1024:(fg + 1) * 1024, half * 512:(half + 1) * 512].rearrange("(c p) n -> p c n", p=128), 8, f"wbff2{fg}{half}"

        wst = {"next_load": 0, "next_use": 0, "rest_ready": False}

        def issue_load(li, key):
            src, nk, res = wsrc(key)
            sl = li % NSLOT
            dma("sp", ring[:, sl, 0:nk, :], src, [res], [f"ring{sl}"], chan=f"ring{sl}")

        def use_slot(key):
            idx = wst["next_use"]
            wst["next_use"] += 1
            plan_out["seq"].append(key)
            if plan:
                seq = plan["seq"]
                assert seq[idx] == key, (idx, key, seq[idx])
                while wst["next_load"] <= idx + NSLOT - 1 and wst["next_load"] < len(seq):
                    if seq[wst["next_load"]][0] != "in" and not wst["rest_ready"]:
                        break
                    issue_load(wst["next_load"], seq[wst["next_load"]])
                    wst["next_load"] += 1
            else:
                issue_load(idx, key)
            sl = idx % NSLOT
            return ring[:, sl], f"ring{sl}"

        def conv_weights(which, gate):
            if which == "in":
                for j in range(8):
                    dma("pool", wb_in[:, j * 512:(j + 1) * 512], w_in[:, j * 512:(j + 1) * 512], gate, [f"wbin{j}"], chan=f"cv_in{j}")
                dma("pool", wdt[:], w_in[:, 4096:4112].rearrange("(c p) n -> p c n", p=128), [], ["wdt"], chan="cv_dt")
                return
            for half in range(2):
                dma("pool", wb_out[0:1024, half * 512:(half + 1) * 512], w_out[0:1024, half * 512:(half + 1) * 512], gate, [f"wbout{half}a"], chan=f"cv_out{half}a")
                dma("pool", wb_out[1024:1536, half * 512:(half + 1) * 512], w_out[1024:1536, half * 512:(half + 1) * 512], gate, [f"wbout{half}b"], chan=f"cv_out{half}b")
            for fq in range(4):
                for fs in (2 * fq, 2 * fq + 1):
                    dma("pool", wb_ff1[:, fs * 512:(fs + 1) * 512], w_ff1[:, fs * 512:(fs + 1) * 512], gate, [f"wbff1{fs}"], chan=f"cv_ff1{fs}")
                for half in range(2):
                    dma("pool", wb_ff2[fq * 1024:(fq + 1) * 1024, half * 512:(half + 1) * 512],
                        w_ff2[fq * 1024:(fq + 1) * 1024, half * 512:(half + 1) * 512], gate, [f"wbff2{fq}{half}"], chan=f"cv_ff2{fq}{half}")

        dma("sp", cst[:], consts[:, :], [], ["cst"], chan="setup", final=True)
        dma("sp", vec[:], vecs[:, :], [], ["vec"], chan="setup", final=True)
        dma("sp", rowb[:], rows[0, 0:1072].partition_broadcast(128), [], ["rowb"], chan="setup", final=True)
        convb_stage = ysb[0:1, :].rearrange("p (a b) -> p a b", a=1)
        dma("sp", ysb[0:1, :], rows[0:1, 1072:2096], [], ["ysb"], chan="setup", final=True)
        dma("sp", relu[0:1, 0, :], rows[0:1, 2096:2608], [], ["relu0"], chan="setup", final=True)

        mset("pool", small[:], 0.0, ["small", "ssq0", "ssq1", "ssq2", "ssq3", "rstd0", "rstd1", "rstd2", "rstd3", "gss", "grstd", "A_bc", "eacd", "w2", "rc", "dtr", "adiff", "eacd0", "eacd1", "w20", "w21"])
        mset("pool", epsc, EPS, ["small"])
        mset("pool", onec, 1.0, ["small"])
        mset("pool", ones_f[:], 1.0, ["ones_f"])
        mset("pool", ones_bf[:], 1.0, ["ones_bf"])
        conv_weights("in", [])
        mset("pool", Vaug[:], 1.0, ["Vaug"])
        mset("pool", qT2[:], 0.0, ["qT"])
        mset("pool", ovl[:], 0.0, ["uT", "xs0", "xs1", "xs2", "xs3"])
        mset("pool", Sst[:], 0.0, ["Sst"])
        mset("pool", Sbf[:], 0.0, ["Sbf"])
        cp("dve", ident_bf[:], ident_f, ["cst"], ["ident_bf"])
        cp("dve", Tm_bf[:], Tm_f, ["cst"], ["Tm_bf"])
        cp("dve", Um_bf[:], Um_f, ["cst"], ["Um_bf"])
        cp("dve", convb_bf[0:1, 0:1024], ysb[0:1, :], ["ysb"], ["convb_bf"])
        cp("dve", convb_bf[0:1, 1024:1536], relu[0:1, 0, :], ["relu0"], ["convb_bf"])
        act(A_bc, alog_bc, AF.Exp, ["rowb", "small"], ["A_bc"])
        tsc("dve", A_bc, A_bc, -1.0, None, ALU.mult, None, ["A_bc"], ["A_bc"])
        for c in range(12):
            for j in range(4):
                tsc("dve", diagW[:, c, j, :], ident_f, vec[:, 36 + c * 4 + j:37 + c * 4 + j], None, ALU.mult, None, ["cst", "vec"], ["diagW"])

        def setup_bias_tables():
            relu_flat = relu[:].rearrange("p a b -> p (a b)")
            for hh in range(8):
                if hh % 2 == 0:
                    stage, sres, ch = ysb[:, 0:640], "ysb", "bt0"
                else:
                    stage, sres, ch = relu_flat[:, 0:640], "relu0", "bt1"
                wr = [sres] if hh % 2 == 0 else ["relu0", "relu1"]
                dma("sp", stage, btab[:, hh * 640:(hh + 1) * 640], [], wr, chan=ch)
                act(expBT[:, hh].rearrange("p a b -> p (a b)"), stage, AF.Copy, wr, ["expBT"])
            mset("pool", expBT[64:128, :, 4, 0:64], -30000.0, ["expBT"])
            mset("pool", expBT[0:64, :, 0, 64:128], -30000.0, ["expBT"])

        def rms_to_T(t, from_stage, wcol, tagbanks):
            yield mset("pool", ssq, 0.0, ["ssq0", "ssq1", "ssq2", "ssq3"])
            for s in range(4):
                b = s % 2
                if from_stage:
                    r0 = (t * 4 + s) * 128
                    yield dma("sp", xst[:, b, :], x[r0:r0 + 128, :], [], [f"xst{b}", "hid"], chan=f"xst{b}")
                    src, sres = xst[:, b, :], f"xst{b}"
                else:
                    src, sres = h[:, s, :], f"h{s}"
                yield act(junk[:], src, AF.Square, [sres, f"ssq{s}"], ["junk", f"ssq{s}"], accum_out=ssq[:, s:s + 1])
                yield tsc("dve", rstd[:, s:s + 1], ssq[:, s:s + 1], 1.0 / D, EPS, ALU.mult, ALU.add, [f"ssq{s}"], [f"rstd{s}"])
                yield act(rstd[:, s:s + 1], rstd[:, s:s + 1], AF.Ln, [f"rstd{s}"], [f"rstd{s}"])
                yield act(rstd[:, s:s + 1], rstd[:, s:s + 1], AF.Exp, [f"rstd{s}"], [f"rstd{s}"], scale=-0.5)
                yield act(xsbf[:, b, :], src, AF.Copy, [sres, f"rstd{s}"], [f"xsbf{b}"], scale=rstd[:, s:s + 1])
                pb = tagbanks[b]
                pv = bank_bf(pb).rearrange("p (c n) -> p c n", c=8)
                yield tr_group([(pv[:, c, :], xsbf[:, b, c * 128:(c + 1) * 128]) for c in range(8)], [f"xsbf{b}", "ident_bf"], pres(pb))
                yield tt("dve", hnT[:, :, s * 128:(s + 1) * 128], pv, vec[:, wcol:wcol + 8].unsqueeze(2).to_broadcast([128, 8, 128]), ALU.mult,
                         pres(pb) + ["vec"], ["hnT"])

        rot = {"i": 0}

        def next_bank(lo=2, n=6):
            b = lo + rot["i"] % n
            rot["i"] += 1
            return b

        def in_proj(t):
            kcol = (t % 2) * 512
            for j in range(8):
                slot, sres = use_slot(("in", j))
                if j == 5 and t > 0:
                    yield cp("act", uT[:, :, 0:3], uhist[:], ["uhist"], ["uT"])
                if j in (0, 1, 5, 6, 7):
                    for cc in range(4):
                        b = next_bank()
                        yield mm_group([(bank(b), slot[:, k, cc * 128:(cc + 1) * 128], hnT[:, k, :], k == 0, k == 7) for k in range(8)],
                                       [sres, "hnT"], pres(b))
                        if j == 0:
                            yield act(qT2[0:64, 0, cc, :], bank(b)[0:64, :], AF.Copy, pres(b), ["qT"], scale=0.125)
                            yield act(qT2[64:128, 1, cc, :], bank(b)[64:128, :], AF.Copy, pres(b), ["qT"], scale=0.125)
                        elif j == 1:
                            yield cp("dve", kT[:, cc, kcol:kcol + 512], bank(b), pres(b), ["kT"])
                        else:
                            c = (j - 5) * 4 + cc
                            yield cp("act", uT[:, c, 3:515], bank(b), pres(b), ["uT"])
                else:
                    for s in range(4):
                        b = next_bank()
                        yield mm_group([(bank(b), hnT[:, k, s * 128:(s + 1) * 128], slot[:, k, :], k == 0, k == 7) for k in range(8)],
                                       [sres, "hnT"], pres(b))
                        if j == 2:
                            blk = (t * 4 + s) % 8
                            yield cp("dve", Vaug[:, blk, :, 0:64], bank(b).rearrange("p (a d) -> p a d", a=8), pres(b), ["Vaug"])
                        else:
                            yield act(zs[:, s, (j - 3) * 512:(j - 2) * 512], bank(b), AF.Silu, pres(b), ["zs"])
            yield cp("act", uhist[:], uT[:, :, 512:515], ["uT"], ["uhist"])
            b = next_bank()
            for s in range(4):
                yield mm_group([(bank(b)[:, s * 16:(s + 1) * 16], hnT[:, k, s * 128:(s + 1) * 128], wdt[:, k, :], k == 0, k == 7) for k in range(8)],
                               ["wdt", "hnT"], pres(b))
            yield tt("dve", dtr, bank(b)[:, 0:64].rearrange("p (s n) -> p s n", s=4), dtb_bc.unsqueeze(1).to_broadcast([128, 4, 16]), ALU.add,
                     pres(b) + ["rowb"], ["dtr"])
            yield act(dtr, dtr, AF.Exp, ["dtr"], ["dtr"])
            yield act(dtt[:], dtr, AF.Ln, ["dtr", "small"], ["dtt"], bias=onec)
            yield tt("dve", aa[:], dtt[:], A_bc.unsqueeze(1).to_broadcast([128, 4, 16]), ALU.mult, ["dtt", "A_bc"], ["aa"])

        def conv_fm(t):
            for ci, c in enumerate(range(8, 12)):
                b = next_bank()
                yield mm_group([(bank(b), diagW[:, c, j, :], uT[:, c, j:j + 512], j == 0, j == 3) for j in range(4)], ["diagW", "uT"], pres(b))
                yield act(BCT[:, ci, :], bank(b), AF.Silu, pres(b) + ["vec"], ["BCT"], bias=vec[:, 24 + c:25 + c])

        prog = {}

        def conv_tm_all(t):
            for s_ in range(4):
                xres = f"xs{s_}"
                for g, (c0, c1) in enumerate(((0, 4), (4, 8), (8, 10))):
                    b_ = next_bank()
                    items = []
                    for c in range(c0, c1):
                        reg = bank(b_)[:, (c - c0) * 128:(c - c0 + 1) * 128]
                        items.append((reg, ones_bf[0:1, :], convb_bf[0:1, c * 128:(c + 1) * 128], True, False))
                        for j in range(4):
                            items.append((reg, uT[:, c, s_ * 128 + j:s_ * 128 + j + 128], diagW[:, c, j, :], False, j == 3))
                    yield mm_group(items, ["ones_bf", "convb_bf", "uT", "diagW"], pres(b_))
                    n = (c1 - c0) * 128
                    yield act(xs_tm[:, s_, c0 * 128:c0 * 128 + n], bank(b_)[:, 0:n], AF.Silu, pres(b_), [xres])

        def ssd_pre(t):
            b_ = next_bank()
            items = []
            for s_ in range(4):
                items.append((bank(b_)[:, s_ * 32:s_ * 32 + 16], Tm_f, aa[:, s_, :], True, True))
                items.append((bank(b_)[:, s_ * 32 + 16:s_ * 32 + 32], ones_f[:], aa[:, s_, :], True, True))
            yield mm_group(items, ["cst", "ones_f", "aa"], pres(b_))
            pv4 = bank(b_)[:, 0:128].rearrange("p (s n) -> p s n", s=4)
            yield cp("act", raw4[:], pv4, pres(b_), ["raw4"])
            yield tt("dve", w24[:], raw4[:, :, 16:32], raw4[:, :, 0:16], ALU.subtract, ["raw4"], ["w24"])
            yield act(eacd4[:], raw4[:], AF.Exp, ["raw4"], ["eacd4"])
            yield act(w24[:], w24[:], AF.Exp, ["w24"], ["w24"])
            yield tt("dve", w24[:], w24[:], dtt[:], ALU.mult, ["w24", "dtt"], ["w24"])
            yield cp("dve", ahl4[:, 0], aa[:], ["aa"], ["ahl4"])
            yield tt("dve", ahl4[:, 1], aa[:], ahl4[:, 0], ALU.subtract, ["aa", "ahl4"], ["ahl4"])

        def ssd_front(t, s):
            xb = s
            xsx3 = xs_tm[:, xb, 0:1024].rearrange("p (a d) -> p a d", a=16)
            xres = f"xs{xb}"
            cs = s * 128
            for r in range(2):
                for hl in range(2):
                    yield tt("dve" if hl == 0 else "pool", AU[:, hl], Um_bf[:].unsqueeze(1).to_broadcast([128, 8, 128]),
                             ahl4[:, hl, s, r * 8:(r + 1) * 8].unsqueeze(2).to_broadcast([128, 8, 128]), ALU.mult, ["Um_bf", "ahl4"], [f"AU{hl}"])
                for q in range(2):
                    bq = 1
                    items = []
                    for i in range(4):
                        reg = bank(bq)[:, i * 128:(i + 1) * 128]
                        items.append((reg, AU[:, 0, q * 4 + i, :], Tm_bf[:], True, False))
                        items.append((reg, AU[:, 1, q * 4 + i, :], Tm_bf[:], False, True))
                    yield mm_group(items, ["AU0", "AU1", "Tm_bf"], pres(bq))
                    yield act(DT[:, r * 8 + q * 4:r * 8 + q * 4 + 4, :], bank(bq).rearrange("p (a n) -> p a n", a=4), AF.Exp, pres(bq), ["DT"])
            yield mm_group([(bank(1)[:, g * 128:128 + g * 128], BCT[:, g, cs:cs + 128], BCT[:, 2 + g, cs:cs + 128], True, True) for g in range(2)],
                           ["BCT"], ["ps1"])
            yield tt("dve", CBm[:], bank(1)[:, 0:256].rearrange("p (g n) -> p g n", g=2), Tm_f.unsqueeze(1).to_broadcast([128, 2, 128]), ALU.mult,
                     ["ps1", "cst"], ["CBm"])
            while prog.get(("By", t), 0) < s:
                yield WAIT
            yield tt("dve", MT[:].rearrange("p (g a) n -> p g a n", g=2), DT[:].rearrange("p (g a) n -> p g a n", g=2),
                     CBm[:].unsqueeze(2).to_broadcast([128, 2, 8, 128]), ALU.mult, ["DT", "CBm"], ["MT"])
            yield tt("pool", Xdt[:].rearrange("p (a d) -> p a d", a=16), xsx3, dtt[:, s, :].unsqueeze(2).to_broadcast([128, 16, 64]), ALU.mult,
                     [xres, "dtt"], ["Xdt"])
            yield tt("pool", Xsk[:].rearrange("p (a d) -> p a d", a=16), xsx3, dsk_bc.unsqueeze(2).to_broadcast([128, 16, 64]), ALU.mult,
                     [xres, "rowb"], ["Xsk"])
            prog[("F", t)] = s + 1

        def ssd_tail(t, s):
            cs = s * 128
            pv = bank_bf(2).rearrange("p (c n) -> p c n", c=8)
            yield tr_group([(pv[:, c, :], yn[:, c * 128:(c + 1) * 128]) for c in range(8)], ["yn", "ident_bf"], pres(2))
            yield tt("dve", mixT[:, 4:12, cs:cs + 128], pv, vec[:, 8:16].unsqueeze(2).to_broadcast([128, 8, 128]), ALU.mult, pres(2) + ["vec"], ["mixT"])

        def ssd_back(t, s):
            xb = s
            xsx3 = xs_tm[:, xb, 0:1024].rearrange("p (a d) -> p a d", a=16)
            xres = f"xs{xb}"
            cs = s * 128
            ea_cd = eacd4[:, s, :]
            w2b = w24[:, s, :]
            yield tt("pool", Xd[:].rearrange("p (a d) -> p a d", a=16), xsx3, w2b.unsqueeze(2).to_broadcast([128, 16, 64]), ALU.mult,
                     [xres, "w24"], ["Xd"])
            for g in range(2):
                bg = 2 if g == 0 else 0
                yield mm_group([(bank(bg), BCT[:, 2 + g, cs:cs + 128], Sbf[:, g * 512:(g + 1) * 512], True, True)], ["BCT", "Sbf"], pres(bg))
            for g in range(2):
                bg = 2 if g == 0 else 0
                ys = ysb[:, g * 512:(g + 1) * 512]
                yield tt("dve", ys.rearrange("p (a d) -> p a d", a=8), bank(bg).rearrange("p (a d) -> p a d", a=8),
                         ea_cd[:, g * 8:(g + 1) * 8].unsqueeze(2).to_broadcast([128, 8, 64]), ALU.mult, pres(bg) + ["eacd4"], ["ysb"])
            while prog.get(("F", t), 0) < s + 1:
                yield WAIT
            for g in range(2):
                bg = 2 if g == 0 else 0
                items = [(bank(bg), ident_bf[:], Xsk[:, g * 512:(g + 1) * 512], True, False)]
                for hh in range(8):
                    hd = g * 8 + hh
                    items.append((bank(bg)[:, hh * 64:(hh + 1) * 64], MT[:, hd, :], Xdt[:, hd * 64:(hd + 1) * 64], False, True))
                yield mm_group(items, ["ident_bf", "Xsk", "MT", "Xdt"], pres(bg))
            for g in range(2):
                bg = 2 if g == 0 else 0
                ys = ysb[:, g * 512:(g + 1) * 512]
                yield tt("dve", ys, ys, bank(bg), ALU.add, ["ysb"] + pres(bg), ["ysb"])
            prog[("By", t)] = s + 1
            if s > 0:
                yield from ssd_tail(t, s - 1)
            yield tt("dve", Sst[:].rearrange("p (a d) -> p a d", a=16), Sst[:].rearrange("p (a d) -> p a d", a=16),
                     ea_cd[:, 16:32].unsqueeze(2).to_broadcast([128, 16, 64]), ALU.mult, ["Sst", "eacd4"], ["Sst"])
            for g in range(2):
                bg = 2 if g == 0 else 0
                yield mm_group([(bank(bg), xs_tm[:, xb, 1024 + g * 128:1152 + g * 128], Xd[:, g * 512:(g + 1) * 512], True, True)], [xres, "Xd"], pres(bg))
            for g in range(2):
                bg = 2 if g == 0 else 0
                yield tt("dve", Sst[:, g * 512:(g + 1) * 512], Sst[:, g * 512:(g + 1) * 512], bank(bg), ALU.add, ["Sst"] + pres(bg), ["Sst"])
            yield cp("act", Sbf[:], Sst[:], ["Sst"], ["Sbf"])
            yield tt("dve", ysb[:], ysb[:], zs[:, s, :], ALU.mult, ["ysb", "zs"], ["ysb"])
            yield mset("pool", gss, 0.0, ["gss"])
            for g in range(2):
                yield act(junk[:, 0:512], ysb[:, g * 512:(g + 1) * 512], AF.Square, ["ysb", "gss"], ["junk", "gss"], accum_out=gss[:, g:g + 1])
            yield tsc("dve", grstd, gss, 1.0 / 512, EPS, ALU.mult, ALU.add, ["gss"], ["grstd"])
            yield act(grstd, grstd, AF.Ln, ["grstd"], ["grstd"])
            yield act(grstd, grstd, AF.Exp, ["grstd"], ["grstd"], scale=-0.5)
            yield tt("dve", yn[:].rearrange("p (g n) -> p g n", g=2), ysb[:].rearrange("p (g n) -> p g n", g=2),
                     grstd.unsqueeze(2).to_broadcast([128, 2, 512]), ALU.mult, ["ysb", "grstd"], ["yn"])
            if s == 3:
                yield from ssd_tail(t, 3)
            prog[("Bdone", t)] = s + 1

        def ssd_F(t):
            for s in range(4):
                yield from ssd_front(t, s)

        def ssd_B(t):
            for s in range(4):
                yield from ssd_back(t, s)

        def att_tail(m):
            pv = bank_bf(3)[:, 0:512].rearrange("p (c n) -> p c n", c=4)
            yield tr_group([(pv[:, c, :], att_tm[:, m % 2, c * 128:(c + 1) * 128]) for c in range(4)], [f"att_tm{m % 2}", "ident_bf"], pres(3))
            yield cp("dve", mixT[:, 0:4, m * 128:(m + 1) * 128], pv, pres(3), ["mixT"])

        def attention(t):
            for m in range(4):
                gm = t * 4 + m
                nv = min(gm, 4) + 1
                gb0 = gm - (nv - 1)
                for hp in range(4):
                    SC = PS[:, 3 * 512:3 * 512 + 1280].rearrange("p (a j n) -> p a j n", a=2, j=5)
                    items = []
                    for hd in range(2):
                        r0 = hd * 64
                        for jj in range(nv):
                            kc = ((gb0 + jj) % 8) * 128
                            items.append((SC[:, hd, jj, :], kT[:, hp, kc:kc + 128], qT2[:, hd, hp, m * 128:(m + 1) * 128], True, False))
                            items.append((SC[:, hd, jj, :], ident_bf[:], expBT[:, 2 * hp + hd, 5 - nv + jj, :], False, True))
                    sres = ["ps3", "ps4", "ps5"]
                    yield mm_group(items, ["kT", "qT", "ident_bf", "expBT"], sres)
                    Ev = E[:, :, 0:nv, :]
                    yield act(Ev, SC[:, :, 0:nv, :], AF.Exp, sres, ["E"])
                    PO = bank(5)[:, 256:386].rearrange("p (a d) -> p a d", a=2)
                    items = []
                    for hd in range(2):
                        for jj in range(nv):
                            items.append((PO[:, hd, :], E[:, hd, jj, :], Vaug[:, (gb0 + jj) % 8, 2 * hp + hd, :], jj == 0, jj == nv - 1))
                    yield mm_group(items, ["E", "Vaug"], ["ps5"])
                    yield R.op("dve", lambda e, PO=PO: e.reciprocal(out=rc, in_=PO[:, :, 64]), ["ps5"], ["rc"])
                    yield tt("dve", att_tm[:, m % 2, hp * 128:(hp + 1) * 128].rearrange("p (a d) -> p a d", a=2), PO[:, :, 0:64],
                             rc.unsqueeze(2).to_broadcast([128, 2, 64]), ALU.mult, ["ps5", "rc"], [f"att_tm{m % 2}"])
                    if hp == 1 and m > 0:
                        yield from att_tail(m - 1)
                if m == 3:
                    yield from att_tail(3)

        def load_h(t):
            for s in range(4):
                r0 = (t * 4 + s) * 128
                yield dma("sp", h[:, s, :], x[r0:r0 + 128, :], [], [f"h{s}"], chan=f"x{s}")

        def out_proj(t):
            for half in range(2):
                slotA, ra = use_slot(("out", half, 0))
                for s in range(4):
                    yield mm_group([(bank(s), mixT[:, k, s * 128:(s + 1) * 128], slotA[:, k, :], k == 0, False) for k in range(8)], [ra, "mixT"], pres(s))
                slotB, rb = use_slot(("out", half, 1))
                for s in range(4):
                    yield mm_group([(bank(s), mixT[:, 8 + k, s * 128:(s + 1) * 128], slotB[:, k, :], False, k == 3) for k in range(4)], [rb, "mixT"], pres(s))
                    hs = h[:, s, half * 512:(half + 1) * 512]
                    yield tt("dve", hs, hs, bank(s), ALU.add, [f"h{s}"] + pres(s), [f"h{s}"])

        def ffn(t):
            yield from rms_to_T(t, False, 16, (6, 7))
            i = 0
            for fq in range(4):
                for fsi in range(2):
                    slot, sres = use_slot(("ff1", fq * 2 + fsi))
                    for fc in range(4):
                        b = 6 + i % 2
                        rb = i % 2
                        i += 1
                        yield mm_group([(bank(b), slot[:, k, fc * 128:(fc + 1) * 128], hnT[:, k, :], k == 0, k == 7) for k in range(8)], [sres, "hnT"], pres(b))
                        yield act(relu[:, rb, :], bank(b), AF.Relu, pres(b), [f"relu{rb}"])
                        yield tt("pool", hidT[:, fsi * 4 + fc, :], relu[:, rb, :], relu[:, rb, :], ALU.mult, [f"relu{rb}"], ["hid", "xst0", "xst1"])
                for half in range(2):
                    slot, sres = use_slot(("ff2", fq, half))
                    for s in range(4):
                        b = 6 + i % 2
                        i += 1
                        rb = i % 2
                        yield mm_group([(bank(b), hidT[:, k, s * 128:(s + 1) * 128], slot[:, k, :], k == 0, k == 7) for k in range(8)], [sres, "hid"], pres(b))
                        hs = h[:, s, half * 512:(half + 1) * 512]
                        if FFN2_VIA_POOL:
                            yield cp("act", relu[:, rb, :], bank(b), pres(b), [f"relu{rb}"])
                            yield tt("pool", hs, hs, relu[:, rb, :], ALU.add, [f"h{s}", f"relu{rb}"], [f"h{s}"])
                        else:
                            yield tt("dve", hs, hs, bank(b), ALU.add, [f"h{s}"] + pres(b), [f"h{s}"])
            yield from final_norm_store(t)

        lastout = {}

        def final_norm_store(t):
            yield mset("pool", ssq, 0.0, ["ssq0", "ssq1", "ssq2", "ssq3"])
            for s in range(4):
                yield act(junk[:], h[:, s, :], AF.Square, [f"h{s}", f"ssq{s}"], ["junk", f"ssq{s}"], accum_out=ssq[:, s:s + 1])
                yield tsc("dve", rstd[:, s:s + 1], ssq[:, s:s + 1], 1.0 / D, EPS, ALU.mult, ALU.add, [f"ssq{s}"], [f"rstd{s}"])
                yield act(rstd[:, s:s + 1], rstd[:, s:s + 1], AF.Ln, [f"rstd{s}"], [f"rstd{s}"])
                yield act(rstd[:, s:s + 1], rstd[:, s:s + 1], AF.Exp, [f"rstd{s}"], [f"rstd{s}"], scale=-0.5)
                yield R.op("dve", lambda e, s=s: e.scalar_tensor_tensor(out=h[:, s, :], in0=h[:, s, :], scalar=rstd[:, s:s + 1], in1=nfw_bc, op0=ALU.mult, op1=ALU.mult),
                           [f"h{s}", f"rstd{s}", "rowb"], [f"h{s}"])
                r0 = (t * 4 + s) * 128
                lastout[s] = dma("sp", y[r0:r0 + 128, :], h[:, s, :], [f"h{s}"], [], chan=f"out{s}")
                yield lastout[s]

        def dump(name, ap, shape, res):
            o = dout(name, shape)
            dma("pool", o, ap, res, [], chan="dbg", final=True)

        rms_done = set()

        def ffn_plus(t):
            yield from ffn(t)
            if t + 2 < NT:
                rms_done.add(t + 2)
                yield from rms_to_T(t + 2, True, 0, (6, 7))

        def phaseA(t):
            if t not in rms_done:
                yield from rms_to_T(t, True, 0, (0, 1))
            yield from in_proj(t)
            yield from conv_fm(t)
            yield from conv_tm_all(t)
            yield from ssd_pre(t)

        run(rms_to_T(0, True, 0, (0, 1)))
        run(in_proj(0))
        setup_bias_tables()
        run(conv_fm(0))
        run(conv_tm_all(0))
        run(ssd_pre(0))
        conv_weights("rest", ["BCT"])
        wst["rest_ready"] = True
        interleave(("B", 0), {"ssdF": ssd_F(0), "ssdB": ssd_B(0), "att": attention(0)})
        for t in range(NT):
            if t + 1 < NT:
                run(phaseA(t + 1))
            run(load_h(t))
            if debug and t == debug - 1:
                dump("d_mixT", mixT[:], [128, 12, 512], ["mixT"])
            run(out_proj(t))
            if debug and t == debug - 1:
                dump("d_h1", h[:], [128, 4, 1024], ["h0", "h1", "h2", "h3"])
            if t + 1 < NT:
                if SEQ_MODE == 1:
                    run(ffn(t)); interleave(("C", t), {"ssdF": ssd_F(t + 1), "ssdB": ssd_B(t + 1)}); run(attention(t + 1))
                elif SEQ_MODE == 2:
                    interleave(("C", t), {"ffn": ffn(t), "att": attention(t + 1)}); run(ssd_all(t + 1))
                elif SEQ_MODE == 3:
                    interleave(("C", t), {"ffn": ffn(t), "ssd": ssd_all(t + 1)}); run(attention(t + 1))
                else:
                    interleave(("C", t), {"ffn": ffn_plus(t), "ssdF": ssd_F(t + 1), "ssdB": ssd_B(t + 1), "att": attention(t + 1)})
            else:
                run(ffn(t))

        fin = R.op("sp", lambda e: e.nop(), [], [])
        for o in lastout.values():
            fin.deps.add(o)
        for e in ENGS:
            for o in R.ops[e]:
                if o.chan == "dbg":
                    fin.deps.add(o)
        R.emit_all(nc, es)
    return nc, dbg, plan_out


def host_layout(inputs, NT=8):
    f = lambda a: np.ascontiguousarray(np.asarray(a, dtype=np.float32))
    S = NT * 512
    w_in = f(inputs["w_in"][0])
    w_out = f(inputs["w_out"][0])
    w_ff1 = f(inputs["w_ff1"][0])
    w_ff2 = f(inputs["w_ff2"][0])
    fm = lambda v: f(v).reshape(-1, 128).T
    conv_w = f(inputs["conv_w"][0])
    convw_fm = conv_w.reshape(4, 12, 128).transpose(2, 1, 0).reshape(128, 48)
    vecs = np.concatenate([fm(inputs["norm_mix_w"][0]), fm(inputs["ssd_norm_w"][0]), fm(inputs["norm_mlp_w"][0]),
                           fm(inputs["conv_b"][0]), convw_fm], axis=1)
    rows = np.concatenate([f(inputs["norm_final_w"]), f(inputs["dt_bias"][0]), f(inputs["a_log"][0]), f(inputs["d_skip"][0]),
                           f(inputs["conv_b"][0])])[None, :]
    rb = f(inputs["rel_bias"][0])
    ki = np.arange(128)[:, None, None]
    jj = np.arange(5)[None, :, None]
    qi = np.arange(128)[None, None, :]
    idx = np.clip(128 * (4 - jj) + qi - ki, -256, 256) + 256
    btab = rb[idx]
    btab = np.ascontiguousarray(btab.transpose(0, 3, 1, 2)).reshape(128, 8 * 640)
    k = np.arange(128)
    ident = np.eye(128, dtype=np.float32)
    Tm = (k[:, None] <= k[None, :]).astype(np.float32)
    Um = (k[:, None] > k[None, :]).astype(np.float32)
    consts = np.concatenate([ident, Tm, Um], axis=1)
    common = {"w_in": w_in, "w_out": w_out, "w_ff1": w_ff1, "w_ff2": w_ff2, "vecs": f(vecs), "rows": f(rows),
              "btab": f(btab), "consts": f(consts)}
    xs = f(inputs["x"])
    return [dict(common, x=np.ascontiguousarray(xs[b, :S])) for b in range(xs.shape[0])]


_CACHE = {}


def kernel(**inputs):
    NT = 8
    in_maps = host_layout(inputs, NT)
    if "nc" not in _CACHE:
        _CACHE["nc"] = build_program(NT)[0]
    nc = _CACHE["nc"]
    res = run_bass_kernel_spmd(nc, in_maps, core_ids=list(range(8)))
    out = np.stack([np.asarray(r["y"]) for r in res.results], axis=0)
    return out.astype(np.float32)
```

```python
import numpy as np
from contextlib import ExitStack
import concourse.bass as bass
import concourse.mybir as mybir
from concourse.bass_utils import run_bass_kernel_spmd

F32 = mybir.dt.float32
BF16 = mybir.dt.bfloat16
AF = mybir.ActivationFunctionType
ALU = mybir.AluOpType

D = 1024
INP = 4112
EPS = 1e-5
NSLOT = 3
import os
FFN2_VIA_POOL = int(os.environ.get('FFN2_VIA_POOL', '0'))
LEAD = float(os.environ.get('LEAD', '1.15'))
NOTILE = int(os.environ.get('NOTILE', '0'))
SEQ_MODE = int(os.environ.get('SEQ_MODE', '0'))
ENGS = ("pe", "act", "dve", "pool", "sp")


class Op:
    __slots__ = ("eng", "emit", "deps", "chan", "val", "milestone", "ms", "final")

    def __init__(self, eng, emit, chan):
        self.eng, self.emit, self.chan = eng, emit, chan
        self.deps = set()
        self.val = 0
        self.milestone = False
        self.ms = 0
        self.final = False


class Rec:
    def __init__(self):
        self.ops = {e: [] for e in ENGS}
        self.lastw = {}
        self.readers = {}
        self.chan = {}
        self.final_chans = set()

    def op(self, eng, emit, reads=(), writes=(), chan=None, final=False):
        o = Op(eng, emit, chan)
        deps = set()
        for r in reads:
            w = self.lastw.get(r)
            if w is not None:
                deps.add(w)
        for w_ in writes:
            w = self.lastw.get(w_)
            if w is not None:
                deps.add(w)
            for rd in self.readers.get(w_, {}).values():
                deps.add(rd)
        if eng == "pe":
            deps = {d for d in deps if not (d.eng == "pe" and d.chan is None)}
        o.deps = deps
        key = eng if chan is None else ("dma", chan)
        for r in reads:
            self.readers.setdefault(r, {})[key] = o
        for w_ in writes:
            self.lastw[w_] = o
            self.readers[w_] = {}
        if chan is not None:
            n = self.chan.get(chan, 0) + 1
            self.chan[chan] = n
            o.val = 16 * n
            o.final = final
            if final:
                self.final_chans.add(chan)
        self.ops[eng].append(o)
        return o

    def emit_all(self, nc, es):
        for e in ENGS:
            for o in self.ops[e]:
                for d in o.deps:
                    d.milestone = True
        engsem = {e: es.enter_context(nc.semaphore("sem_" + e)) for e in ENGS}
        chansem = {c: es.enter_context(nc.semaphore("ch_" + str(c))) for c in self.chan}
        for e in ENGS:
            k = 0
            for o in self.ops[e]:
                if o.chan is None and o.milestone:
                    k += 1
                    o.ms = k
        block = es.enter_context(nc.Block())
        rec = self

        def run(e, eng):
            waited = {}
            for o in rec.ops[e]:
                for d in o.deps:
                    if d.chan is not None:
                        sem = chansem[d.chan]
                        val = 16 * rec.chan[d.chan] if d.final else d.val
                    else:
                        sem = engsem[d.eng]
                        val = d.ms
                    kk = sem.num
                    if waited.get(kk, 0) < val:
                        eng.wait_ge(sem, val)
                        waited[kk] = val
                ins = o.emit(eng)
                if o.chan is not None:
                    ins.then_inc(chansem[o.chan], 16)
                elif o.milestone:
                    ins.then_inc(engsem[e], 1)

        @block.sync
        def _(eng):
            run("sp", eng)

        @block.tensor
        def _(eng):
            run("pe", eng)

        @block.scalar
        def _(eng):
            run("act", eng)

        @block.vector
        def _(eng):
            run("dve", eng)

        @block.gpsimd
        def _(eng):
            run("pool", eng)


def build_program(NT=8, debug=False):
    _, _, plan = _build(NT, debug, None)
    nc, dbg, _ = _build(NT, debug, plan)
    return nc, dbg


def _build(NT, debug, plan):
    S = NT * 512
    nc = bass.Bass("TRN2", target_bir_lowering=False)
    R = Rec()
    plan_out = {"seq": [], "counts": {}}

    def din(name, shape, dt=F32):
        return nc.dram_tensor(name, list(shape), dt, kind="ExternalInput").ap()

    x = din("x", [S, D])
    w_in = din("w_in", [D, INP])
    w_out = din("w_out", [1536, D])
    w_ff1 = din("w_ff1", [D, 4096])
    w_ff2 = din("w_ff2", [4096, D])
    vecs = din("vecs", [128, 84])
    rows = din("rows", [1, 2608])
    btab = din("btab", [128, 8 * 640])
    consts = din("consts", [128, 384])
    y = nc.dram_tensor("y", [S, D], F32, kind="ExternalOutput").ap()
    wb_in = nc.dram_tensor("wb_in", [D, 4096], BF16, kind="Internal").ap()
    wb_out = nc.dram_tensor("wb_out", [1536, D], BF16, kind="Internal").ap()
    wb_ff1 = nc.dram_tensor("wb_ff1", [D, 4096], BF16, kind="Internal").ap()
    wb_ff2 = nc.dram_tensor("wb_ff2", [4096, D], BF16, kind="Internal").ap()
    dbg = {}

    def dout(name, shape):
        a = nc.dram_tensor(name, list(shape), F32, kind="ExternalOutput").ap()
        dbg[name] = a
        return a

    es = ExitStack()
    with es:
        def sb(name, shape, dt):
            return es.enter_context(nc.sbuf_tensor(name, list(shape), dt))

        ring = sb("ring", [128, NSLOT, 8, 512], BF16)
        h = sb("h", [128, 4, 1024], F32)
        hnT = sb("hnT", [128, 8, 512], BF16)
        qT2 = sb("qT2", [128, 2, 4, 512], BF16)
        kT = sb("kT", [128, 4, 1024], BF16)
        Vaug = sb("Vaug", [128, 8, 8, 65], BF16)
        zs = sb("zs", [128, 4, 1024], BF16)
        ovl = sb("ovl", [128, 11312], BF16)
        uT = ovl[:, 0:6192].rearrange("p (c n) -> p c n", c=12)
        xs_tm = ovl[:, 6192:11312].rearrange("p (b n) -> p b n", b=4)
        hidT = sb("hidT", [128, 8, 512], BF16)
        xst = hidT[:].rearrange("p a b -> p (a b)").bitcast(F32).rearrange("p (b n) -> p b n", b=2)
        BCT = sb("BCT", [128, 4, 512], BF16)
        AU = sb("AU", [128, 2, 8, 128], BF16)
        DT = sb("DT", [128, 16, 128], BF16)
        MT = sb("MT", [128, 16, 128], BF16)
        CBm = sb("CBm", [128, 2, 128], BF16)
        Xdt = sb("Xdt", [128, 1024], BF16)
        Xd = sb("Xd", [128, 1024], BF16)
        Sst = sb("Sst", [128, 1024], F32)
        Sbf = sb("Sbf", [128, 1024], BF16)
        ysb = sb("ysb", [128, 1024], F32)
        yn = sb("yn", [128, 1024], BF16)
        Xsk = sb("Xsk", [128, 1024], BF16)
        E = sb("E", [128, 2, 5, 128], BF16)
        att_tm = sb("att_tm", [128, 2, 512], BF16)
        mixT = sb("mixT", [128, 12, 512], BF16)
        relu = sb("relu", [128, 2, 512], F32)
        expBT = sb("expBT", [128, 8, 5, 128], BF16)
        diagW = sb("diagW", [128, 12, 4, 128], BF16)
        cst = sb("cst", [128, 384], F32)
        vec = sb("vec", [128, 84], F32)
        rowb = sb("rowb", [128, 1072], F32)
        convb_bf = sb("convb_bf", [1, 1536], BF16)
        ones_bf = sb("ones_bf", [1, 128], BF16)
        ones_f = sb("ones_f", [128, 128], F32)
        ident_bf = sb("ident_bf", [128, 128], BF16)
        Tm_bf = sb("Tm_bf", [128, 128], BF16)
        Um_bf = sb("Um_bf", [128, 128], BF16)
        wdt = sb("wdt", [128, 8, 16], BF16)
        junk = sb("junk", [128, 1024], BF16)
        xsbf = sb("xsbf", [128, 2, 1024], BF16)
        small = sb("small", [128, 256], F32)
        dtt = sb("dtt", [128, 4, 16], F32)
        aa = sb("aa", [128, 4, 16], F32)
        ahl = sb("ahl", [128, 2, 16], BF16)
        uhist = sb("uhist", [128, 12, 3], BF16)
        eacd4 = sb("eacd4", [128, 4, 32], F32)
        w24 = sb("w24", [128, 4, 16], F32)
        raw4 = sb("raw4", [128, 4, 32], F32)
        ahl4 = sb("ahl4", [128, 2, 4, 16], BF16)
        PS = es.enter_context(nc.psum_tensor("PS", [128, 4096], F32))

        ident_f = cst[:, 0:128]
        Tm_f = cst[:, 128:256]
        Um_f = cst[:, 256:384]
        nfw_bc = rowb[:, 0:1024]
        dtb_bc = rowb[:, 1024:1040]
        alog_bc = rowb[:, 1040:1056]
        dsk_bc = rowb[:, 1056:1072]
        ssq = small[:, 0:4]
        rstd = small[:, 4:8]
        gss = small[:, 8:10]
        grstd = small[:, 10:12]
        epsc = small[:, 12:13]
        onec = small[:, 13:14]
        A_bc = small[:, 16:32]
        eacd = small[:, 32:64]
        w2 = small[:, 64:80]
        rc = small[:, 80:82]
        adiff = small[:, 160:176]
        eacd2 = small[:, 176:240].rearrange("p (b n) -> p b n", b=2)
        w22 = small[:, 32:64].rearrange("p (b n) -> p b n", b=2)
        dtr = small[:, 96:160].rearrange("p (s n) -> p s n", s=4)

        def bank(b):
            return PS[:, b * 512:(b + 1) * 512]

        def pres(b):
            return ["ps%d" % b]

        def bank_bf(b):
            return PS[:, b * 512:(b + 1) * 512].bitcast(BF16)

        def mm_group(items, reads, writes):
            def emit(e):
                ins = None
                for (o, l, r, st, sp) in items:
                    ins = e.matmul(o, lhsT=l, rhs=r, start=st, stop=sp, skip_group_check=True)
                return ins
            return R.op("pe", emit, reads, writes)

        def tr_group(items, reads, writes):
            def emit(e):
                ins = None
                for (o, i) in items:
                    ins = e.transpose(out=o, in_=i, identity=ident_bf[:])
                return ins
            return R.op("pe", emit, reads, writes)

        def act(out, in_, func, reads, writes, **kw):
            return R.op("act", lambda e: e.activation(out=out, in_=in_, func=func, **kw), reads, writes)

        def tt(eng, out, in0, in1, op, reads, writes):
            return R.op(eng, lambda e: e.tensor_tensor(out=out, in0=in0, in1=in1, op=op), reads, writes)

        def tsc(eng, out, in0, s1, s2, op0, op1, reads, writes):
            if s2 is None:
                return R.op(eng, lambda e: e.tensor_scalar(out=out, in0=in0, scalar1=s1, scalar2=None, op0=op0), reads, writes)
            return R.op(eng, lambda e: e.tensor_scalar(out=out, in0=in0, scalar1=s1, scalar2=s2, op0=op0, op1=op1), reads, writes)

        def cp(eng, out, in_, reads, writes):
            if eng == "act":
                return R.op(eng, lambda e: e.activation(out=out, in_=in_, func=AF.Copy), reads, writes)
            return R.op(eng, lambda e: e.tensor_copy(out=out, in_=in_), reads, writes)

        def mset(eng, ap, val, writes):
            return R.op(eng, lambda e: e.memset(ap, val), (), writes)

        def dma(eng, out, in_, reads, writes, chan, final=False):
            return R.op(eng, lambda e: e.dma_start(out=out, in_=in_), reads, writes, chan=chan, final=final)

        def run(gen):
            for _ in gen:
                pass

        WAIT = "WAIT"

        def interleave(key, streams):
            names = list(streams)
            cnt = {n: 0 for n in names}
            alive = dict(streams)
            tot = plan["counts"].get(key) if plan else None
            while alive:
                if tot:
                    order = sorted(alive, key=lambda k: (cnt[k] + 0.5) / max(tot[k] * (1.0 if k == "ffn" else LEAD), 1))
                else:
                    order = sorted(alive, key=lambda k: cnt[k])
                progressed = False
                for n in order:
                    try:
                        r = next(alive[n])
                    except StopIteration:
                        del alive[n]
                        progressed = True
                        break
                    if r is WAIT:
                        continue
                    cnt[n] += 1
                    progressed = True
                    break
                assert progressed, ("interleave deadlock", key, list(alive))
            plan_out["counts"][key] = cnt

        def wsrc(key):
            kind = key[0]
            if kind == "in":
                j = key[1]
                return wb_in[:, j * 512:(j + 1) * 512].rearrange("(c p) n -> p c n", p=128), 8, f"wbin{j}"
            if kind == "out":
                half, part = key[1], key[2]
                if part == 0:
                    return wb_out[0:1024, half * 512:(half + 1) * 512].rearrange("(c p) n -> p c n", p=128), 8, f"wbout{half}a"
                return wb_out[1024:1536, half * 512:(half + 1) * 512].rearrange("(c p) n -> p c n", p=128), 4, f"wbout{half}b"
            if kind == "ff1":
                fs = key[1]
                return wb_ff1[:, fs * 512:(fs + 1) * 512].rearrange("(c p) n -> p c n", p=128), 8, f"wbff1{fs}"
            fg, half = key[1], key[2]
            return wb_ff2[fg * 1024:(fg + 1) * 1024, half * 512:(half + 1) * 512].rearrange("(c p) n -> p c n", p=128), 8, f"wbff2{fg}{half}"

        wst = {"next_load": 0, "next_use": 0, "rest_ready": False}

        def issue_load(li, key):
            src, nk, res = wsrc(key)
            sl = li % NSLOT
            dma("sp", ring[:, sl, 0:nk, :], src, [res], [f"ring{sl}"], chan=f"ring{sl}")

        def use_slot(key):
            idx = wst["next_use"]
            wst["next_use"] += 1
            plan_out["seq"].append(key)
            if plan:
                seq = plan["seq"]
                assert seq[idx] == key, (idx, key, seq[idx])
                while wst["next_load"] <= idx + NSLOT - 1 and wst["next_load"] < len(seq):
                    if seq[wst["next_load"]][0] != "in" and not wst["rest_ready"]:
                        break
                    issue_load(wst["next_load"], seq[wst["next_load"]])
                    wst["next_load"] += 1
            else:
                issue_load(idx, key)
            sl = idx % NSLOT
            return ring[:, sl], f"ring{sl}"

        def conv_weights(which, gate):
            if which == "in":
                for j in range(8):
                    dma("pool", wb_in[:, j * 512:(j + 1) * 512], w_in[:, j * 512:(j + 1) * 512], gate, [f"wbin{j}"], chan=f"cv_in{j}")
                dma("pool", wdt[:], w_in[:, 4096:4112].rearrange("(c p) n -> p c n", p=128), [], ["wdt"], chan="cv_dt")
                return
            for half in range(2):
                dma("pool", wb_out[0:1024, half * 512:(half + 1) * 512], w_out[0:1024, half * 512:(half + 1) * 512], gate, [f"wbout{half}a"], chan=f"cv_out{half}a")
                dma("pool", wb_out[1024:1536, half * 512:(half + 1) * 512], w_out[1024:1536, half * 512:(half + 1) * 512], gate, [f"wbout{half}b"], chan=f"cv_out{half}b")
            for fq in range(4):
                for fs in (2 * fq, 2 * fq + 1):
                    dma("pool", wb_ff1[:, fs * 512:(fs + 1) * 512], w_ff1[:, fs * 512:(fs + 1) * 512], gate, [f"wbff1{fs}"], chan=f"cv_ff1{fs}")
                for half in range(2):
                    dma("pool", wb_ff2[fq * 1024:(fq + 1) * 1024, half * 512:(half + 1) * 512],
                        w_ff2[fq * 1024:(fq + 1) * 1024, half * 512:(half + 1) * 512], gate, [f"wbff2{fq}{half}"], chan=f"cv_ff2{fq}{half}")

        dma("sp", cst[:], consts[:, :], [], ["cst"], chan="setup", final=True)
        dma("sp", vec[:], vecs[:, :], [], ["vec"], chan="setup", final=True)
        dma("sp", rowb[:], rows[0, 0:1072].partition_broadcast(128), [], ["rowb"], chan="setup", final=True)
        convb_stage = ysb[0:1, :].rearrange("p (a b) -> p a b", a=1)
        dma("sp", ysb[0:1, :], rows[0:1, 1072:2096], [], ["ysb"], chan="setup", final=True)
        dma("sp", relu[0:1, 0, :], rows[0:1, 2096:2608], [], ["relu0"], chan="setup", final=True)

        mset("pool", small[:], 0.0, ["small", "ssq0", "ssq1", "ssq2", "ssq3", "rstd0", "rstd1", "rstd2", "rstd3", "gss", "grstd", "A_bc", "eacd", "w2", "rc", "dtr", "adiff", "eacd0", "eacd1", "w20", "w21"])
        mset("pool", epsc, EPS, ["small"])
        mset("pool", onec, 1.0, ["small"])
        mset("pool", ones_f[:], 1.0, ["ones_f"])
        mset("pool", ones_bf[:], 1.0, ["ones_bf"])
        conv_weights("in", [])
        mset("pool", Vaug[:], 1.0, ["Vaug"])
        mset("pool", qT2[:], 0.0, ["qT"])
        mset("pool", ovl[:], 0.0, ["uT", "xs0", "xs1", "xs2", "xs3"])
        mset("pool", Sst[:], 0.0, ["Sst"])
        mset("pool", Sbf[:], 0.0, ["Sbf"])
        cp("dve", ident_bf[:], ident_f, ["cst"], ["ident_bf"])
        cp("dve", Tm_bf[:], Tm_f, ["cst"], ["Tm_bf"])
        cp("dve", Um_bf[:], Um_f, ["cst"], ["Um_bf"])
        cp("dve", convb_bf[0:1, 0:1024], ysb[0:1, :], ["ysb"], ["convb_bf"])
        cp("dve", convb_bf[0:1, 1024:1536], relu[0:1, 0, :], ["relu0"], ["convb_bf"])
        act(A_bc, alog_bc, AF.Exp, ["rowb", "small"], ["A_bc"])
        tsc("dve", A_bc, A_bc, -1.0, None, ALU.mult, None, ["A_bc"], ["A_bc"])
        for c in range(12):
            for j in range(4):
                tsc("dve", diagW[:, c, j, :], ident_f, vec[:, 36 + c * 4 + j:37 + c * 4 + j], None, ALU.mult, None, ["cst", "vec"], ["diagW"])

        def setup_bias_tables():
            relu_flat = relu[:].rearrange("p a b -> p (a b)")
            for hh in range(8):
                if hh % 2 == 0:
                    stage, sres, ch = ysb[:, 0:640], "ysb", "bt0"
                else:
                    stage, sres, ch = relu_flat[:, 0:640], "relu0", "bt1"
                wr = [sres] if hh % 2 == 0 else ["relu0", "relu1"]
                dma("sp", stage, btab[:, hh * 640:(hh + 1) * 640], [], wr, chan=ch)
                act(expBT[:, hh].rearrange("p a b -> p (a b)"), stage, AF.Copy, wr, ["expBT"])
            mset("pool", expBT[64:128, :, 4, 0:64], -30000.0, ["expBT"])
            mset("pool", expBT[0:64, :, 0, 64:128], -30000.0, ["expBT"])

        def rms_to_T(t, from_stage, wcol, tagbanks):
            yield mset("pool", ssq, 0.0, ["ssq0", "ssq1", "ssq2", "ssq3"])
            for s in range(4):
                b = s % 2
                if from_stage:
                    r0 = (t * 4 + s) * 128
                    yield dma("sp", xst[:, b, :], x[r0:r0 + 128, :], [], [f"xst{b}", "hid"], chan=f"xst{b}")
                    src, sres = xst[:, b, :], f"xst{b}"
                else:
                    src, sres = h[:, s, :], f"h{s}"
                yield act(junk[:], src, AF.Square, [sres, f"ssq{s}"], ["junk", f"ssq{s}"], accum_out=ssq[:, s:s + 1])
                yield act(rstd[:, s:s + 1], ssq[:, s:s + 1], AF.Ln, [f"ssq{s}", "small"], [f"rstd{s}"], scale=1.0 / D, bias=epsc)
                yield act(rstd[:, s:s + 1], rstd[:, s:s + 1], AF.Exp, [f"rstd{s}"], [f"rstd{s}"], scale=-0.5)
                yield act(xsbf[:, b, :], src, AF.Copy, [sres, f"rstd{s}"], [f"xsbf{b}"], scale=rstd[:, s:s + 1])
                pb = tagbanks[b]
                pv = bank_bf(pb).rearrange("p (c n) -> p c n", c=8)
                yield tr_group([(pv[:, c, :], xsbf[:, b, c * 128:(c + 1) * 128]) for c in range(8)], [f"xsbf{b}", "ident_bf"], pres(pb))
                yield tt("dve", hnT[:, :, s * 128:(s + 1) * 128], pv, vec[:, wcol:wcol + 8].unsqueeze(2).to_broadcast([128, 8, 128]), ALU.mult,
                         pres(pb) + ["vec"], ["hnT"])

        rot = {"i": 0}

        def next_bank(lo=2, n=6):
            b = lo + rot["i"] % n
            rot["i"] += 1
            return b

        def in_proj(t):
            kcol = (t % 2) * 512
            for j in range(8):
                slot, sres = use_slot(("in", j))
                if j == 5 and t > 0:
                    yield cp("act", uT[:, :, 0:3], uhist[:], ["uhist"], ["uT"])
                if j in (0, 1, 5, 6, 7):
                    for cc in range(4):
                        b = next_bank()
                        yield mm_group([(bank(b), slot[:, k, cc * 128:(cc + 1) * 128], hnT[:, k, :], k == 0, k == 7) for k in range(8)],
                                       [sres, "hnT"], pres(b))
                        if j == 0:
                            yield act(qT2[0:64, 0, cc, :], bank(b)[0:64, :], AF.Copy, pres(b), ["qT"], scale=0.125)
                            yield act(qT2[64:128, 1, cc, :], bank(b)[64:128, :], AF.Copy, pres(b), ["qT"], scale=0.125)
                        elif j == 1:
                            yield cp("dve", kT[:, cc, kcol:kcol + 512], bank(b), pres(b), ["kT"])
                        else:
                            c = (j - 5) * 4 + cc
                            yield cp("act", uT[:, c, 3:515], bank(b), pres(b), ["uT"])
                else:
                    for s in range(4):
                        b = next_bank()
                        yield mm_group([(bank(b), hnT[:, k, s * 128:(s + 1) * 128], slot[:, k, :], k == 0, k == 7) for k in range(8)],
                                       [sres, "hnT"], pres(b))
                        if j == 2:
                            blk = (t * 4 + s) % 8
                            yield cp("dve", Vaug[:, blk, :, 0:64], bank(b).rearrange("p (a d) -> p a d", a=8), pres(b), ["Vaug"])
                        else:
                            yield act(zs[:, s, (j - 3) * 512:(j - 2) * 512], bank(b), AF.Silu, pres(b), ["zs"])
            yield cp("act", uhist[:], uT[:, :, 512:515], ["uT"], ["uhist"])
            b = next_bank()
            for s in range(4):
                yield mm_group([(bank(b)[:, s * 16:(s + 1) * 16], hnT[:, k, s * 128:(s + 1) * 128], wdt[:, k, :], k == 0, k == 7) for k in range(8)],
                               ["wdt", "hnT"], pres(b))
            yield tt("dve", dtr, bank(b)[:, 0:64].rearrange("p (s n) -> p s n", s=4), dtb_bc.unsqueeze(1).to_broadcast([128, 4, 16]), ALU.add,
                     pres(b) + ["rowb"], ["dtr"])
            yield act(dtr, dtr, AF.Exp, ["dtr"], ["dtr"])
            yield act(dtt[:], dtr, AF.Ln, ["dtr", "small"], ["dtt"], bias=onec)
            yield tt("dve", aa[:], dtt[:], A_bc.unsqueeze(1).to_broadcast([128, 4, 16]), ALU.mult, ["dtt", "A_bc"], ["aa"])

        def conv_fm(t):
            for ci, c in enumerate(range(8, 12)):
                b = next_bank()
                yield mm_group([(bank(b), diagW[:, c, j, :], uT[:, c, j:j + 512], j == 0, j == 3) for j in range(4)], ["diagW", "uT"], pres(b))
                yield act(BCT[:, ci, :], bank(b), AF.Silu, pres(b) + ["vec"], ["BCT"], bias=vec[:, 24 + c:25 + c])

        prog = {}

        def conv_tm_all(t):
            for s_ in range(4):
                xres = f"xs{s_}"
                for g, (c0, c1) in enumerate(((0, 4), (4, 8), (8, 10))):
                    b_ = next_bank()
                    items = []
                    for c in range(c0, c1):
                        reg = bank(b_)[:, (c - c0) * 128:(c - c0 + 1) * 128]
                        items.append((reg, ones_bf[0:1, :], convb_bf[0:1, c * 128:(c + 1) * 128], True, False))
                        for j in range(4):
                            items.append((reg, uT[:, c, s_ * 128 + j:s_ * 128 + j + 128], diagW[:, c, j, :], False, j == 3))
                    yield mm_group(items, ["ones_bf", "convb_bf", "uT", "diagW"], pres(b_))
                    n = (c1 - c0) * 128
                    yield act(xs_tm[:, s_, c0 * 128:c0 * 128 + n], bank(b_)[:, 0:n], AF.Silu, pres(b_), [xres])

        def ssd_pre(t):
            b_ = next_bank()
            items = []
            for s_ in range(4):
                items.append((bank(b_)[:, s_ * 32:s_ * 32 + 16], Tm_f, aa[:, s_, :], True, True))
                items.append((bank(b_)[:, s_ * 32 + 16:s_ * 32 + 32], ones_f[:], aa[:, s_, :], True, True))
            yield mm_group(items, ["cst", "ones_f", "aa"], pres(b_))
            pv4 = bank(b_)[:, 0:128].rearrange("p (s n) -> p s n", s=4)
            yield cp("act", raw4[:], pv4, pres(b_), ["raw4"])
            yield tt("dve", w24[:], raw4[:, :, 16:32], raw4[:, :, 0:16], ALU.subtract, ["raw4"], ["w24"])
            yield act(eacd4[:], raw4[:], AF.Exp, ["raw4"], ["eacd4"])
            yield act(w24[:], w24[:], AF.Exp, ["w24"], ["w24"])
            yield tt("dve", w24[:], w24[:], dtt[:], ALU.mult, ["w24", "dtt"], ["w24"])
            yield cp("dve", ahl4[:, 0], aa[:], ["aa"], ["ahl4"])
            yield tt("dve", ahl4[:, 1], aa[:], ahl4[:, 0], ALU.subtract, ["aa", "ahl4"], ["ahl4"])

        def ssd_front(t, s):
            xb = s
            xsx3 = xs_tm[:, xb, 0:1024].rearrange("p (a d) -> p a d", a=16)
            xres = f"xs{xb}"
            cs = s * 128
            for r in range(2):
                for hl in range(2):
                    yield tt("dve" if hl == 0 else "pool", AU[:, hl], Um_bf[:].unsqueeze(1).to_broadcast([128, 8, 128]),
                             ahl4[:, hl, s, r * 8:(r + 1) * 8].unsqueeze(2).to_broadcast([128, 8, 128]), ALU.mult, ["Um_bf", "ahl4"], [f"AU{hl}"])
                for q in range(2):
                    bq = 1
                    items = []
                    for i in range(4):
                        reg = bank(bq)[:, i * 128:(i + 1) * 128]
                        items.append((reg, AU[:, 0, q * 4 + i, :], Tm_bf[:], True, False))
                        items.append((reg, AU[:, 1, q * 4 + i, :], Tm_bf[:], False, True))
                    yield mm_group(items, ["AU0", "AU1", "Tm_bf"], pres(bq))
                    yield act(DT[:, r * 8 + q * 4:r * 8 + q * 4 + 4, :], bank(bq).rearrange("p (a n) -> p a n", a=4), AF.Exp, pres(bq), ["DT"])
            yield mm_group([(bank(1)[:, g * 128:128 + g * 128], BCT[:, g, cs:cs + 128], BCT[:, 2 + g, cs:cs + 128], True, True) for g in range(2)],
                           ["BCT"], ["ps1"])
            yield tt("dve", CBm[:], bank(1)[:, 0:256].rearrange("p (g n) -> p g n", g=2), Tm_f.unsqueeze(1).to_broadcast([128, 2, 128]), ALU.mult,
                     ["ps1", "cst"], ["CBm"])
            while prog.get(("By", t), 0) < s:
                yield WAIT
            yield tt("dve", MT[:].rearrange("p (g a) n -> p g a n", g=2), DT[:].rearrange("p (g a) n -> p g a n", g=2),
                     CBm[:].unsqueeze(2).to_broadcast([128, 2, 8, 128]), ALU.mult, ["DT", "CBm"], ["MT"])
            yield tt("pool", Xdt[:].rearrange("p (a d) -> p a d", a=16), xsx3, dtt[:, s, :].unsqueeze(2).to_broadcast([128, 16, 64]), ALU.mult,
                     [xres, "dtt"], ["Xdt"])
            yield tt("pool", Xsk[:].rearrange("p (a d) -> p a d", a=16), xsx3, dsk_bc.unsqueeze(2).to_broadcast([128, 16, 64]), ALU.mult,
                     [xres, "rowb"], ["Xsk"])
            prog[("F", t)] = s + 1

        def ssd_tail(t, s):
            cs = s * 128
            pv = bank_bf(2).rearrange("p (c n) -> p c n", c=8)
            yield tr_group([(pv[:, c, :], yn[:, c * 128:(c + 1) * 128]) for c in range(8)], ["yn", "ident_bf"], pres(2))
            yield tt("dve", mixT[:, 4:12, cs:cs + 128], pv, vec[:, 8:16].unsqueeze(2).to_broadcast([128, 8, 128]), ALU.mult, pres(2) + ["vec"], ["mixT"])

        def ssd_back(t, s):
            xb = s
            xsx3 = xs_tm[:, xb, 0:1024].rearrange("p (a d) -> p a d", a=16)
            xres = f"xs{xb}"
            cs = s * 128
            ea_cd = eacd4[:, s, :]
            w2b = w24[:, s, :]
            yield tt("pool", Xd[:].rearrange("p (a d) -> p a d", a=16), xsx3, w2b.unsqueeze(2).to_broadcast([128, 16, 64]), ALU.mult,
                     [xres, "w24"], ["Xd"])
            for g in range(2):
                bg = 2 if g == 0 else 0
                yield mm_group([(bank(bg), BCT[:, 2 + g, cs:cs + 128], Sbf[:, g * 512:(g + 1) * 512], True, True)], ["BCT", "Sbf"], pres(bg))
            for g in range(2):
                bg = 2 if g == 0 else 0
                ys = ysb[:, g * 512:(g + 1) * 512]
                yield tt("dve", ys.rearrange("p (a d) -> p a d", a=8), bank(bg).rearrange("p (a d) -> p a d", a=8),
                         ea_cd[:, g * 8:(g + 1) * 8].unsqueeze(2).to_broadcast([128, 8, 64]), ALU.mult, pres(bg) + ["eacd4"], ["ysb"])
            while prog.get(("F", t), 0) < s + 1:
                yield WAIT
            for g in range(2):
                bg = 2 if g == 0 else 0
                items = [(bank(bg), ident_bf[:], Xsk[:, g * 512:(g + 1) * 512], True, False)]
                for hh in range(8):
                    hd = g * 8 + hh
                    items.append((bank(bg)[:, hh * 64:(hh + 1) * 64], MT[:, hd, :], Xdt[:, hd * 64:(hd + 1) * 64], False, True))
                yield mm_group(items, ["ident_bf", "Xsk", "MT", "Xdt"], pres(bg))
            for g in range(2):
                bg = 2 if g == 0 else 0
                ys = ysb[:, g * 512:(g + 1) * 512]
                yield tt("dve", ys, ys, bank(bg), ALU.add, ["ysb"] + pres(bg), ["ysb"])
            prog[("By", t)] = s + 1
            if s > 0:
                yield from ssd_tail(t, s - 1)
            yield tt("dve", Sst[:].rearrange("p (a d) -> p a d", a=16), Sst[:].rearrange("p (a d) -> p a d", a=16),
                     ea_cd[:, 16:32].unsqueeze(2).to_broadcast([128, 16, 64]), ALU.mult, ["Sst", "eacd4"], ["Sst"])
            for g in range(2):
                bg = 2 if g == 0 else 0
                yield mm_group([(bank(bg), xs_tm[:, xb, 1024 + g * 128:1152 + g * 128], Xd[:, g * 512:(g + 1) * 512], True, True)], [xres, "Xd"], pres(bg))
            for g in range(2):
                bg = 2 if g == 0 else 0
                yield tt("dve", Sst[:, g * 512:(g + 1) * 512], Sst[:, g * 512:(g + 1) * 512], bank(bg), ALU.add, ["Sst"] + pres(bg), ["Sst"])
            yield cp("act", Sbf[:], Sst[:], ["Sst"], ["Sbf"])
            yield tt("dve", ysb[:], ysb[:], zs[:, s, :], ALU.mult, ["ysb", "zs"], ["ysb"])
            yield mset("pool", gss, 0.0, ["gss"])
            for g in range(2):
                yield act(junk[:, 0:512], ysb[:, g * 512:(g + 1) * 512], AF.Square, ["ysb", "gss"], ["junk", "gss"], accum_out=gss[:, g:g + 1])
            yield act(grstd, gss, AF.Ln, ["gss", "small"], ["grstd"], scale=1.0 / 512, bias=epsc)
            yield act(grstd, grstd, AF.Exp, ["grstd"], ["grstd"], scale=-0.5)
            for g in range(2):
                yield act(yn[:, g * 512:(g + 1) * 512], ysb[:, g * 512:(g + 1) * 512], AF.Copy, ["ysb", "grstd"], ["yn"], scale=grstd[:, g:g + 1])
            if s == 3:
                yield from ssd_tail(t, 3)
            prog[("Bdone", t)] = s + 1

        def ssd_F(t):
            for s in range(4):
                yield from ssd_front(t, s)

        def ssd_B(t):
            for s in range(4):
                yield from ssd_back(t, s)

        def att_tail(m):
            pv = bank_bf(3)[:, 0:512].rearrange("p (c n) -> p c n", c=4)
            yield tr_group([(pv[:, c, :], att_tm[:, m % 2, c * 128:(c + 1) * 128]) for c in range(4)], [f"att_tm{m % 2}", "ident_bf"], pres(3))
            yield cp("dve", mixT[:, 0:4, m * 128:(m + 1) * 128], pv, pres(3), ["mixT"])

        def attention(t):
            for m in range(4):
                gm = t * 4 + m
                nv = min(gm, 4) + 1
                gb0 = gm - (nv - 1)
                for hp in range(4):
                    SC = PS[:, 3 * 512:3 * 512 + 1280].rearrange("p (a j n) -> p a j n", a=2, j=5)
                    items = []
                    for hd in range(2):
                        r0 = hd * 64
                        for jj in range(nv):
                            kc = ((gb0 + jj) % 8) * 128
                            items.append((SC[:, hd, jj, :], kT[:, hp, kc:kc + 128], qT2[:, hd, hp, m * 128:(m + 1) * 128], True, False))
                            items.append((SC[:, hd, jj, :], ident_bf[:], expBT[:, 2 * hp + hd, 5 - nv + jj, :], False, True))
                    sres = ["ps3", "ps4", "ps5"]
                    yield mm_group(items, ["kT", "qT", "ident_bf", "expBT"], sres)
                    Ev = E[:, :, 0:nv, :]
                    yield act(Ev, SC[:, :, 0:nv, :], AF.Exp, sres, ["E"])
                    PO = bank(5)[:, 256:386].rearrange("p (a d) -> p a d", a=2)
                    items = []
                    for hd in range(2):
                        for jj in range(nv):
                            items.append((PO[:, hd, :], E[:, hd, jj, :], Vaug[:, (gb0 + jj) % 8, 2 * hp + hd, :], jj == 0, jj == nv - 1))
                    yield mm_group(items, ["E", "Vaug"], ["ps5"])
                    yield R.op("dve", lambda e, PO=PO: e.reciprocal(out=rc, in_=PO[:, :, 64]), ["ps5"], ["rc"])
                    yield tt("dve", att_tm[:, m % 2, hp * 128:(hp + 1) * 128].rearrange("p (a d) -> p a d", a=2), PO[:, :, 0:64],
                             rc.unsqueeze(2).to_broadcast([128, 2, 64]), ALU.mult, ["ps5", "rc"], [f"att_tm{m % 2}"])
                    if hp == 1 and m > 0:
                        yield from att_tail(m - 1)
                if m == 3:
                    yield from att_tail(3)

        def load_h(t):
            for s in range(4):
                r0 = (t * 4 + s) * 128
                yield dma("sp", h[:, s, :], x[r0:r0 + 128, :], [], [f"h{s}"], chan=f"x{s}")

        def out_proj(t):
            for half in range(2):
                slotA, ra = use_slot(("out", half, 0))
                for s in range(4):
                    yield mm_group([(bank(s), mixT[:, k, s * 128:(s + 1) * 128], slotA[:, k, :], k == 0, False) for k in range(8)], [ra, "mixT"], pres(s))
                slotB, rb = use_slot(("out", half, 1))
                for s in range(4):
                    yield mm_group([(bank(s), mixT[:, 8 + k, s * 128:(s + 1) * 128], slotB[:, k, :], False, k == 3) for k in range(4)], [rb, "mixT"], pres(s))
                    hs = h[:, s, half * 512:(half + 1) * 512]
                    yield tt("dve", hs, hs, bank(s), ALU.add, [f"h{s}"] + pres(s), [f"h{s}"])

        def ffn(t):
            yield from rms_to_T(t, False, 16, (6, 7))
            i = 0
            for fq in range(4):
                for fsi in range(2):
                    slot, sres = use_slot(("ff1", fq * 2 + fsi))
                    for fc in range(4):
                        b = 6 + i % 2
                        rb = i % 2
                        i += 1
                        yield mm_group([(bank(b), slot[:, k, fc * 128:(fc + 1) * 128], hnT[:, k, :], k == 0, k == 7) for k in range(8)], [sres, "hnT"], pres(b))
                        yield act(relu[:, rb, :], bank(b), AF.Relu, pres(b), [f"relu{rb}"])
                        yield tt("pool", hidT[:, fsi * 4 + fc, :], relu[:, rb, :], relu[:, rb, :], ALU.mult, [f"relu{rb}"], ["hid", "xst0", "xst1"])
                for half in range(2):
                    slot, sres = use_slot(("ff2", fq, half))
                    for s in range(4):
                        b = 6 + i % 2
                        i += 1
                        rb = i % 2
                        yield mm_group([(bank(b), hidT[:, k, s * 128:(s + 1) * 128], slot[:, k, :], k == 0, k == 7) for k in range(8)], [sres, "hid"], pres(b))
                        hs = h[:, s, half * 512:(half + 1) * 512]
                        if FFN2_VIA_POOL:
                            yield cp("act", relu[:, rb, :], bank(b), pres(b), [f"relu{rb}"])
                            yield tt("pool", hs, hs, relu[:, rb, :], ALU.add, [f"h{s}", f"relu{rb}"], [f"h{s}"])
                        else:
                            yield tt("dve", hs, hs, bank(b), ALU.add, [f"h{s}"] + pres(b), [f"h{s}"])
            yield from final_norm_store(t)

        lastout = {}

        def final_norm_store(t):
            yield mset("pool", ssq, 0.0, ["ssq0", "ssq1", "ssq2", "ssq3"])
            for s in range(4):
                yield act(junk[:], h[:, s, :], AF.Square, [f"h{s}", f"ssq{s}"], ["junk", f"ssq{s}"], accum_out=ssq[:, s:s + 1])
                yield act(rstd[:, s:s + 1], ssq[:, s:s + 1], AF.Ln, [f"ssq{s}", "small"], [f"rstd{s}"], scale=1.0 / D, bias=epsc)
                yield act(rstd[:, s:s + 1], rstd[:, s:s + 1], AF.Exp, [f"rstd{s}"], [f"rstd{s}"], scale=-0.5)
                yield R.op("dve", lambda e, s=s: e.scalar_tensor_tensor(out=h[:, s, :], in0=h[:, s, :], scalar=rstd[:, s:s + 1], in1=nfw_bc, op0=ALU.mult, op1=ALU.mult),
                           [f"h{s}", f"rstd{s}", "rowb"], [f"h{s}"])
                r0 = (t * 4 + s) * 128
                lastout[s] = dma("sp", y[r0:r0 + 128, :], h[:, s, :], [f"h{s}"], [], chan=f"out{s}")
                yield lastout[s]

        def dump(name, ap, shape, res):
            o = dout(name, shape)
            dma("pool", o, ap, res, [], chan="dbg", final=True)

        rms_done = set()

        def ffn_plus(t):
            yield from ffn(t)
            if t + 2 < NT:
                rms_done.add(t + 2)
                yield from rms_to_T(t + 2, True, 0, (6, 7))

        def phaseA(t):
            if t not in rms_done:
                yield from rms_to_T(t, True, 0, (0, 1))
            yield from in_proj(t)
            yield from conv_fm(t)
            yield from conv_tm_all(t)
            yield from ssd_pre(t)

        run(rms_to_T(0, True, 0, (0, 1)))
        run(in_proj(0))
        setup_bias_tables()
        run(conv_fm(0))
        run(conv_tm_all(0))
        run(ssd_pre(0))
        conv_weights("rest", ["BCT"])
        wst["rest_ready"] = True
        interleave(("B", 0), {"ssdF": ssd_F(0), "ssdB": ssd_B(0), "att": attention(0)})
        for t in range(NT):
            if t + 1 < NT:
                run(phaseA(t + 1))
            run(load_h(t))
            if debug and t == debug - 1:
                dump("d_mixT", mixT[:], [128, 12, 512], ["mixT"])
            run(out_proj(t))
            if debug and t == debug - 1:
                dump("d_h1", h[:], [128, 4, 1024], ["h0", "h1", "h2", "h3"])
            if t + 1 < NT:
                if SEQ_MODE == 1:
                    run(ffn(t)); interleave(("C", t), {"ssdF": ssd_F(t + 1), "ssdB": ssd_B(t + 1)}); run(attention(t + 1))
                elif SEQ_MODE == 2:
                    interleave(("C", t), {"ffn": ffn(t), "att": attention(t + 1)}); run(ssd_all(t + 1))
                elif SEQ_MODE == 3:
                    interleave(("C", t), {"ffn": ffn(t), "ssd": ssd_all(t + 1)}); run(attention(t + 1))
                else:
                    interleave(("C", t), {"ffn": ffn_plus(t), "ssdF": ssd_F(t + 1), "ssdB": ssd_B(t + 1), "att": attention(t + 1)})
            else:
                run(ffn(t))

        fin = R.op("sp", lambda e: e.nop(), [], [])
        for o in lastout.values():
            fin.deps.add(o)
        for e in ENGS:
            for o in R.ops[e]:
                if o.chan == "dbg":
                    fin.deps.add(o)
        R.emit_all(nc, es)
    return nc, dbg, plan_out


def host_layout(inputs, NT=8):
    f = lambda a: np.ascontiguousarray(np.asarray(a, dtype=np.float32))
    S = NT * 512
    w_in = f(inputs["w_in"][0])
    w_out = f(inputs["w_out"][0])
    w_ff1 = f(inputs["w_ff1"][0])
    w_ff2 = f(inputs["w_ff2"][0])
    fm = lambda v: f(v).reshape(-1, 128).T
    conv_w = f(inputs["conv_w"][0])
    convw_fm = conv_w.reshape(4, 12, 128).transpose(2, 1, 0).reshape(128, 48)
    vecs = np.concatenate([fm(inputs["norm_mix_w"][0]), fm(inputs["ssd_norm_w"][0]), fm(inputs["norm_mlp_w"][0]),
                           fm(inputs["conv_b"][0]), convw_fm], axis=1)
    rows = np.concatenate([f(inputs["norm_final_w"]), f(inputs["dt_bias"][0]), f(inputs["a_log"][0]), f(inputs["d_skip"][0]),
                           f(inputs["conv_b"][0])])[None, :]
    rb = f(inputs["rel_bias"][0])
    ki = np.arange(128)[:, None, None]
    jj = np.arange(5)[None, :, None]
    qi = np.arange(128)[None, None, :]
    idx = np.clip(128 * (4 - jj) + qi - ki, -256, 256) + 256
    btab = rb[idx]
    btab = np.ascontiguousarray(btab.transpose(0, 3, 1, 2)).reshape(128, 8 * 640)
    k = np.arange(128)
    ident = np.eye(128, dtype=np.float32)
    Tm = (k[:, None] <= k[None, :]).astype(np.float32)
    Um = (k[:, None] > k[None, :]).astype(np.float32)
    consts = np.concatenate([ident, Tm, Um], axis=1)
    common = {"w_in": w_in, "w_out": w_out, "w_ff1": w_ff1, "w_ff2": w_ff2, "vecs": f(vecs), "rows": f(rows),
              "btab": f(btab), "consts": f(consts)}
    xs = f(inputs["x"])
    return [dict(common, x=np.ascontiguousarray(xs[b, :S])) for b in range(xs.shape[0])]


_CACHE = {}


def kernel(**inputs):
    NT = 8
    in_maps = host_layout(inputs, NT)
    if "nc" not in _CACHE:
        _CACHE["nc"] = build_program(NT)[0]
    nc = _CACHE["nc"]
    res = run_bass_kernel_spmd(nc, in_maps, core_ids=list(range(8)))
    out = np.stack([np.asarray(r["y"]) for r in res.results], axis=0)
    return out.astype(np.float32)
```
